# Optimizing a Trainium2 kernel written in Bass

```python
import jax, jax.numpy as jnp
from jax import lax
import numpy as np

D_MODEL = 1024
BATCH = 16
SEQ = 2048
DEPTH = 4
DEC_BATCH = 8
DEC_SEQ = 4096
PAST_LEN = 128

GRID_W = 64
HG_HEADS = 4
HG_DK = 128
HG_DV = 128
HG_KEY_WIDTH = HG_HEADS * HG_DK
HG_WIDTH = HG_HEADS * HG_DV
HG_CHUNK = 32
NA_HEADS = 8
NA_DH = 64
NA_WIDTH = NA_HEADS * NA_DH
WIN_ROWS = 8
WIN_COLS = 16
MIX_WIDTH = HG_WIDTH + NA_WIDTH
D_FF = 256 * ((8 * D_MODEL // 3 + 255) // 256)
EPS = 1e-6
IN_SIZES = (HG_KEY_WIDTH, HG_KEY_WIDTH, HG_KEY_WIDTH, HG_WIDTH, HG_WIDTH, NA_WIDTH, NA_WIDTH, NA_WIDTH)
IN_WIDTH = sum(IN_SIZES)

kernel_name = "hybrid_hgrn2_natten_macaron_adaln_encoder"


def _rms_norm(x, g):
    xf = x.astype(jnp.float32)
    y = xf * lax.rsqrt(jnp.mean(xf * xf, axis=-1, keepdims=True) + EPS)
    return (y * g.astype(jnp.float32)).astype(x.dtype)


def _modulate(n, m):
    return n * (1 + m[:, 1]) + m[:, 0]


def _swiglu(h, w_gate, w_up, w_down):
    return (jax.nn.silu(h @ w_gate) * (h @ w_up)) @ w_down


def _lower_bounds(p):
    sm = jax.nn.softmax(p.astype(jnp.float32), axis=0)
    return jnp.cumsum(sm, axis=0) - sm[0]


def _gla_chunk_scan(q, k, v, log_f):
    B, L, H, DK = q.shape
    DV = v.shape[-1]
    n = L // HG_CHUNK

    def to_chunks(a):
        return a.reshape(B, n, HG_CHUNK, H, a.shape[-1]).transpose(1, 0, 3, 2, 4)

    incl = jnp.tril(jnp.ones((HG_CHUNK, HG_CHUNK), dtype=bool))

    def step(S, blk):
        qc, kc, vc, gc = blk
        b = jnp.cumsum(gc, axis=2)
        b_last = b[:, :, -1]
        inter = jnp.einsum('bhik,bhkv->bhiv', qc * jnp.exp(b), S)
        rel = jnp.where(incl[:, :, None], b[:, :, :, None, :] - b[:, :, None, :, :], -jnp.inf)
        scores = jnp.einsum('bhik,bhjk,bhijk->bhij', qc, kc, jnp.exp(rel))
        intra = jnp.einsum('bhij,bhjv->bhiv', scores, vc)
        S = jnp.exp(b_last)[..., None] * S + jnp.einsum(
            'bhjk,bhjv->bhkv', kc * jnp.exp(b_last[:, :, None] - b), vc)
        return S, inter + intra

    S0 = jnp.zeros((B, H, DK, DV), jnp.float32)
    _, o = lax.scan(step, S0, (to_chunks(q), to_chunks(k), to_chunks(v), to_chunks(log_f)))
    return o.transpose(1, 0, 3, 2, 4).reshape(B, L, H, DV)


def _hgrn2_bidir(hq, hf_f, hf_b, hi, hg, lb_f, lb_b, norm_g):
    B, L, _ = hq.shape
    q = jax.nn.silu(hq.astype(jnp.float32)).reshape(B, L, HG_HEADS, HG_DK)
    v = hi.astype(jnp.float32).reshape(B, L, HG_HEADS, HG_DV)
    f_f = (lb_f + (1 - lb_f) * jax.nn.sigmoid(hf_f.astype(jnp.float32))).reshape(B, L, HG_HEADS, HG_DK)
    f_b = (lb_b + (1 - lb_b) * jax.nn.sigmoid(hf_b.astype(jnp.float32))).reshape(B, L, HG_HEADS, HG_DK)
    o_f = _gla_chunk_scan(q, 1 - f_f, v, jnp.log(f_f))
    flip = lambda a: jnp.flip(a, axis=1)
    o_b = flip(_gla_chunk_scan(flip(q), flip(1 - f_b), flip(v), flip(jnp.log(f_b))))
    o = _rms_norm(o_f + o_b, norm_g).reshape(B, L, HG_WIDTH)
    return (o * jax.nn.silu(hg.astype(jnp.float32))).astype(hq.dtype)


def _neighbourhood_attention(q, k, v, rpb):
    B, L, H, dh = q.shape
    rows = L // GRID_W
    wr = min(WIN_ROWS, rows)
    grid = lambda a: a.reshape(B, rows, GRID_W, H, dh).transpose(0, 3, 1, 2, 4)
    qg, kg, vg = grid(q), grid(k), grid(v)
    qcol = np.arange(GRID_W)
    c0 = np.clip(qcol - WIN_COLS // 2, 0, GRID_W - WIN_COLS)
    col_mask = jnp.asarray((qcol[None, :] >= c0[:, None]) & (qcol[None, :] < c0[:, None] + WIN_COLS))
    rel_c = jnp.asarray(np.clip(qcol[None, :] - qcol[:, None], -(WIN_COLS - 1), WIN_COLS - 1) + WIN_COLS - 1)
    scale = dh ** -0.5

    def row_block(r):
        r0 = jnp.clip(r - wr // 2, 0, rows - wr)
        kb = lax.dynamic_slice_in_dim(kg, r0, wr, axis=2)
        vb = lax.dynamic_slice_in_dim(vg, r0, wr, axis=2)
        qr = lax.dynamic_index_in_dim(qg, r, axis=2, keepdims=False)
        s = jnp.einsum('bhqd,bhrkd->bhqrk', qr, kb).astype(jnp.float32) * scale
        rel_r = r0 + jnp.arange(wr) - r + (WIN_ROWS - 1)
        bias = rpb[:, rel_r[None, :, None], rel_c[:, None, :]].astype(jnp.float32)
        s = jnp.where(col_mask[:, None, :], s + bias[None], -jnp.inf)
        p = jax.nn.softmax(s, axis=(-2, -1)).astype(vb.dtype)
        return jnp.einsum('bhqrk,bhrkd->bhqd', p, vb)

    out = lax.map(row_block, jnp.arange(rows))
    return out.transpose(1, 0, 3, 2, 4).reshape(B, L, H * dh)


def _mixer(h, w_in_l, w_out_l, lb_f, lb_b, hg_norm_g_l, qn_g, kn_g, rpb_l):
    B, L, _ = h.shape
    proj = h @ w_in_l
    offs = np.cumsum(IN_SIZES)[:-1].tolist()
    hq, hf_f, hf_b, hi, hg, nq, nk, nv = jnp.split(proj, offs, axis=-1)
    o_hg = _hgrn2_bidir(hq, hf_f, hf_b, hi, hg, lb_f, lb_b, hg_norm_g_l)
    heads = lambda a: a.reshape(B, L, NA_HEADS, NA_DH)
    o_na = _neighbourhood_attention(_rms_norm(heads(nq), qn_g), _rms_norm(heads(nk), kn_g), heads(nv), rpb_l)
    return jnp.concatenate([o_hg, o_na], axis=-1) @ w_out_l


def _trunk(x, c, w_mod, b_mod, norm_g, ffn_w_gate, ffn_w_up, ffn_w_down, w_in, w_out,
           lbs_f, lbs_b, hg_norm_g, na_q_norm_g, na_k_norm_g, na_rpb):
    B = x.shape[0]
    for l in range(DEPTH):
        mod = (jax.nn.silu(c) @ w_mod[l] + b_mod[l]).reshape(B, 3, 3, D_MODEL)[:, :, :, None, :]
        h = _modulate(_rms_norm(x, norm_g[l, 0]), mod[:, 0])
        x = x + 0.5 * mod[:, 0, 2] * _swiglu(h, ffn_w_gate[l, 0], ffn_w_up[l, 0], ffn_w_down[l, 0])
        h = _modulate(_rms_norm(x, norm_g[l, 1]), mod[:, 1])
        x = x + mod[:, 1, 2] * _mixer(h, w_in[l], w_out[l], lbs_f[l], lbs_b[l], hg_norm_g[l],
                                      na_q_norm_g[l], na_k_norm_g[l], na_rpb[l])
        h = _modulate(_rms_norm(x, norm_g[l, 2]), mod[:, 2])
        x = x + 0.5 * mod[:, 2, 2] * _swiglu(h, ffn_w_gate[l, 1], ffn_w_up[l, 1], ffn_w_down[l, 1])
    return x


def setup_inputs(seed: int = 0) -> dict:
    key = jax.random.key(seed)
    ks = jax.random.split(key, 20)
    D = D_MODEL
    nrm = lambda k, shape, s: jax.random.normal(k, shape, jnp.float32) * s
    return {
        "x_prompt": nrm(ks[0], (BATCH, SEQ, D), 1.0),
        "x_sample": nrm(ks[1], (DEC_BATCH, DEC_SEQ, D), 1.0),
        "c_prompt": nrm(ks[2], (BATCH, D), 1.0),
        "c_sample": nrm(ks[3], (DEC_BATCH, D), 1.0),
        "w_mod": nrm(ks[4], (DEPTH, D, 9 * D), 0.5 * D ** -0.5),
        "b_mod": nrm(ks[5], (DEPTH, 9 * D), 0.02),
        "norm_g": 1.0 + nrm(ks[6], (DEPTH, 3, D), 0.02),
        "ffn_w_gate": nrm(ks[7], (DEPTH, 2, D, D_FF), D ** -0.5),
        "ffn_w_up": nrm(ks[8], (DEPTH, 2, D, D_FF), D ** -0.5),
        "ffn_w_down": nrm(ks[9], (DEPTH, 2, D_FF, D), D_FF ** -0.5),
        "w_in": nrm(ks[10], (DEPTH, D, IN_WIDTH), D ** -0.5),
        "w_out": nrm(ks[11], (DEPTH, MIX_WIDTH, D), MIX_WIDTH ** -0.5),
        "hg_lb_fwd": nrm(ks[12], (DEPTH, HG_KEY_WIDTH), 1.0),
        "hg_lb_bwd": nrm(ks[13], (DEPTH, HG_KEY_WIDTH), 1.0),
        "hg_norm_g": 1.0 + nrm(ks[14], (DEPTH, HG_DV), 0.02),
        "na_q_norm_g": 1.0 + nrm(ks[15], (DEPTH, NA_DH), 0.02),
        "na_k_norm_g": 1.0 + nrm(ks[16], (DEPTH, NA_DH), 0.02),
        "na_rpb": nrm(ks[17], (DEPTH, NA_HEADS, 2 * WIN_ROWS - 1, 2 * WIN_COLS - 1), 0.1),
    }


def reference(x_prompt, x_sample, c_prompt, c_sample, w_mod, b_mod, norm_g, ffn_w_gate, ffn_w_up,
              ffn_w_down, w_in, w_out, hg_lb_fwd, hg_lb_bwd, hg_norm_g, na_q_norm_g, na_k_norm_g, na_rpb):
    lbs_f = _lower_bounds(hg_lb_fwd)
    lbs_b = _lower_bounds(hg_lb_bwd)
    y_prompt = _trunk(x_prompt, c_prompt, w_mod, b_mod, norm_g, ffn_w_gate, ffn_w_up, ffn_w_down,
                      w_in, w_out, lbs_f, lbs_b, hg_norm_g, na_q_norm_g, na_k_norm_g, na_rpb)
    y_sample = _trunk(x_sample, c_sample, w_mod, b_mod, norm_g, ffn_w_gate, ffn_w_up, ffn_w_down,
                      w_in, w_out, lbs_f, lbs_b, hg_norm_g, na_q_norm_g, na_k_norm_g, na_rpb)
    return (y_prompt, y_sample)
```

```python
import contextlib
import numpy as np
import ml_dtypes
import concourse.bass as bass
import concourse.mybir as mybir
from concourse.bass_utils import run_bass_kernel_spmd

F32 = mybir.dt.float32
BF16 = mybir.dt.bfloat16
AF = mybir.ActivationFunctionType
ALU = mybir.AluOpType
AX = mybir.AxisListType

D = 1024
DFF = 2816
NF = DFF // 128
DEPTH = 4
EPS = 1e-6
INW = 4096
N_CORES = 8
TT = 512

SEM_M = 1 << 30
DSEM_LIM = 1 << 30


class Buf:
    __slots__ = ("name", "t", "last_w", "readers", "excl", "dsem")

    def __init__(self, name, t=None, excl=False):
        self.name = name
        self.t = t
        self.last_w = None
        self.readers = {}
        self.excl = excl
        self.dsem = {}

    def __getitem__(self, k):
        return self.t[k]


class DSem:
    __slots__ = ("sem", "issued", "closed")

    def __init__(self, sem):
        self.sem = sem
        self.issued = 0
        self.closed = False


class Eng:
    def __init__(self, fw, name, eng, is_pe=False):
        self.fw = fw
        self.name = name
        self.eng = eng
        self.is_pe = is_pe
        self.n = 0
        self.sems = []
        self.known = {}

    def sem_for(self, n):
        k = (n - 1) // SEM_M
        while len(self.sems) <= k:
            self.sems.append(self.fw.nc.alloc_semaphore(name=f"s_{self.name}_{len(self.sems)}"))
        return self.sems[k], n - k * SEM_M


class FW:
    def __init__(self, nc):
        self.nc = nc
        self.pe = Eng(self, "pe", nc.tensor, is_pe=True)
        self.act = Eng(self, "act", nc.scalar)
        self.dve = Eng(self, "dve", nc.vector)
        self.pool = Eng(self, "pool", nc.gpsimd)
        self.sp = Eng(self, "sp", nc.sync)
        self.engs = [self.pe, self.act, self.dve, self.pool, self.sp]
        self.dsems = []
        self.free_dsems = {"hw": [], "sw": []}
        self.cur = []
        self.nwaits = 0
        self.nins = 0
        self.ndma = 0
        self._uid = 0

    def sbuf(self, st, name, shape, dt):
        self._uid += 1
        b = Buf(name, st.enter_context(self.nc.sbuf_tensor(f"{name}_{self._uid}", list(shape), dt)))
        self.cur.append(b)
        return b

    def begin_phase(self):
        self.cur = []

    def end_phase(self):
        self.barrier()
        for b in self.cur:
            for kind, ds in b.dsem.items():
                ds.closed = True
                self.free_dsems[kind].append(ds)
            b.dsem = {}
        self.cur = []

    def psum(self, st, name, shape, dt=F32):
        self._uid += 1
        return Buf(name, st.enter_context(self.nc.psum_tensor(f"{name}_{self._uid}", list(shape), dt)), excl=True)

    def dram(self, name, shape, dt, kind="Internal"):
        return self.nc.dram_tensor(name, list(shape), dt, kind=kind).ap()

    def _wait(self, e, tk):
        kind, obj, val = tk
        if kind == "e":
            if obj is e and e.is_pe:
                return
            if e.known.get(obj, 0) >= val:
                return
            e.known[obj] = val
            sem, v = obj.sem_for(val)
            e.eng.wait_ge(sem, v)
            self.nwaits += 1
        else:
            ds = obj
            val = ds.issued
            ds.closed = True
            if e.known.get(ds, 0) >= val:
                return
            e.known[ds] = val
            e.eng.wait_ge(ds.sem, val)
            self.nwaits += 1

    def _deps(self, e, reads, writes):
        need = []
        for b in reads:
            if b.last_w is not None:
                need.append(b.last_w)
            if b.excl:
                for k, t in b.readers.items():
                    if k is not e:
                        need.append(t)
        for b in writes:
            if b.last_w is not None:
                need.append(b.last_w)
            need.extend(b.readers.values())
        for t in need:
            self._wait(e, t)

    def _record(self, tk, key, reads, writes):
        for b in reads:
            b.readers[key] = tk
        for b in writes:
            b.last_w = tk
            b.readers = {}

    def op(self, e, fn, reads=(), writes=()):
        self._deps(e, reads, writes)
        ins = fn()
        e.n += 1
        sem, _ = e.sem_for(e.n)
        ins.then_inc(sem, 1)
        self._record(("e", e, e.n), e, reads, writes)
        self.nins += 1
        return ins

    def dma(self, q, out, in_, reads=(), writes=(), key=None, **kw):
        self._deps(q, reads, writes)
        kind = "sw" if q is self.pool else "hw"
        ds = key.dsem.get(kind)
        if ds is not None and ds.issued + 16 > DSEM_LIM:
            self._wait(q, ("d", ds, ds.issued))
            ds = None
        if ds is None:
            fl = self.free_dsems[kind]
            while fl and ds is None:
                cand = fl.pop()
                if cand.issued + 16 <= DSEM_LIM:
                    ds = cand
            if ds is None:
                self._uid += 1
                ds = DSem(self.nc.alloc_semaphore(name=f"d{kind}_{self._uid}"))
                self.dsems.append(ds)
            key.dsem[kind] = ds
        if ds.closed:
            self._wait(q, ("d", ds, ds.issued))
            ds.closed = False
        ins = q.eng.dma_start(out=out, in_=in_, **kw)
        ins.then_inc(ds.sem, 16)
        ds.issued += 16
        self._record(("d", ds, ds.issued), ds, reads, writes)
        self.ndma += 1
        return ins

    def barrier(self):
        sp = self.sp
        for ds in self.dsems:
            if ds.issued:
                self._wait(sp, ("d", ds, ds.issued))
        for e in self.engs:
            if e is not sp and e.n:
                self._wait(sp, ("e", e, e.n))
        ins = sp.eng.nop()
        sp.n += 1
        sem, _ = sp.sem_for(sp.n)
        ins.then_inc(sem, 1)
        for e in self.engs:
            if e is not sp:
                self._wait(e, ("e", sp, sp.n))

    def finish(self):
        self.barrier()


def _consts():
    c = {}
    c["identF"] = np.eye(128, dtype=np.float32)
    j = np.arange(128)[:, None]
    i = np.arange(128)[None, :]
    same = (j // 32) == (i // 32)
    c["Tf"] = (same & (j <= i)).astype(np.float32)
    c["Tb"] = (same & (j >= i)).astype(np.float32)
    ind = np.zeros((128, 4), np.float32)
    ind[np.arange(128), np.arange(128) // 32] = 1.0
    c["Ind"] = ind
    jj = np.arange(32)[:, None]
    ii = np.arange(32)[None, :]
    m = np.zeros((32, 2, 4, 32), np.float32)
    m[:, 0] = (jj <= ii).astype(np.float32)[:, None, :]
    m[:, 1] = (jj >= ii).astype(np.float32)[:, None, :]
    c["maskA"] = m.reshape(32, 256)
    c["ones"] = np.ones((128, 128), np.float32)
    q = np.arange(64)
    c0 = np.clip(q - 8, 0, 48)
    cm = (q[None, :] >= c0[:, None]) & (q[None, :] < c0[:, None] + 16)
    neg = np.where(cm.T, 0.0, -30000.0).astype(np.float32)
    c["NEG"] = np.concatenate([neg, neg], axis=0)
    return c


def _expand_rpb(rpb):
    L = rpb.shape[0]
    kc = np.arange(64)[:, None]
    qc = np.arange(64)[None, :]
    rel = np.clip(kc - qc, -15, 15) + 15
    G = np.empty((L, 128, 8, 14, 64), np.float32)
    for e in range(2):
        sub = rpb[:, :, e:e + 14, :]
        g = sub[:, :, :, rel]
        G[:, e * 64:(e + 1) * 64] = np.transpose(g, (0, 3, 1, 2, 4))
    return G


class Prog:
    def __init__(self, seqs, depth, stop_after=None):
        self.seqs = list(seqs)
        self.NS = len(seqs)
        self.T = sum(seqs)
        self.depth = depth
        self.stop_after = stop_after
        self.seq_off = [sum(seqs[:i]) for i in range(self.NS)]
        self.nc = bass.Bass("TRN2", target_bir_lowering=False)
        self.fw = FW(self.nc)
        self.dbufs = {}

    def db(self, *key):
        b = self.dbufs.get(key)
        if b is None:
            b = Buf("db_" + "_".join(str(k) for k in key))
            self.dbufs[key] = b
        return b

    def seq_of(self, tok):
        for s in range(self.NS):
            if self.seq_off[s] <= tok < self.seq_off[s] + self.seqs[s]:
                return s
        raise ValueError

    def build(self):
        nc, fw = self.nc, self.fw
        T, NS, depth = self.T, self.NS, self.depth
        dr = fw.dram
        I = "ExternalInput"
        self.x = dr("x", [T, D], F32, I)
        self.c = dr("c", [NS * 8, 128], F32, I)
        self.w_mod = dr("w_mod", [depth, D, 9 * D], F32, I)
        self.b_mod = dr("b_mod", [depth * 72, 128], F32, I)
        self.norm_g = dr("norm_g", [depth * 24, 128], F32, I)
        self.wg = dr("ffn_w_gate", [depth, 2, D, DFF], F32, I)
        self.wu = dr("ffn_w_up", [depth, 2, D, DFF], F32, I)
        self.wd = dr("ffn_w_down", [depth, 2, DFF, D], F32, I)
        self.w_in = dr("w_in", [depth, D, INW], F32, I)
        self.w_out = dr("w_out", [depth, D, D], F32, I)
        self.lbf = dr("hg_lb_fwd", [1, depth * 512], F32, I)
        self.lbb = dr("hg_lb_bwd", [1, depth * 512], F32, I)
        self.hgg = dr("hg_norm_g", [depth, 128], F32, I)
        self.gq = dr("na_q_norm_g", [1, depth * 64], F32, I)
        self.gk = dr("na_k_norm_g", [1, depth * 64], F32, I)
        self.G = dr("G", [depth, 128, 8 * 14 * 64], F32, I)
        cs = _consts()
        self.cd = {k: dr("c_" + k, list(v.shape), F32, I) for k, v in cs.items()}
        self.y = dr("y", [T, D], F32, "ExternalOutput")
        self.xT = dr("xT", [128, 8, T], F32)
        self.qtT = dr("qtT", [2, 128, 4, T], BF16)
        self.ktT = dr("ktT", [2, 128, 4, T], BF16)
        self.ktok = dr("ktok", [2, T, 512], BF16)
        self.vtok = dr("vtok", [T, 512], BF16)
        self.decT = dr("decT", [T // 128, 128, 32], F32)
        self.sgT = dr("sgT", [128, 4, T], BF16)
        self.QT = dr("QT", [64, 8, T], BF16)
        self.KT = dr("KT", [64, 8, T], BF16)
        self.Vna = dr("Vna", [T, 512], BF16)
        self.ohgT = dr("ohgT", [128, 4, T], BF16)
        self.onaT = dr("onaT", [128, 4, T], BF16)
        self.lbD = dr("lbD", [1, depth * 1024], F32)

        with contextlib.ExitStack() as pst:
            self.pst = pst
            self.setup()
            self.phase_in()
            done = False
            for l in range(depth):
                with contextlib.ExitStack() as wst:
                    Wpre = None
                    for sub in range(3):
                        if sub == 1:
                            self.mixer(l, "ABC")
                            if self.stop_after is None or tuple(self.stop_after) != (l, 1):
                                Wpre = self.ffn_weights(wst, l, 1)
                            self.mixer(l, "D")
                        else:
                            self.ffn(l, sub, W=(Wpre if sub == 2 else None))
                        if self.stop_after is not None and (l, sub) == tuple(self.stop_after):
                            done = True
                            break
                if done:
                    break
            self.phase_out()
            fw.finish()
        return nc

    def load_const(self, st, name, shape, dt, src, q=None):
        fw = self.fw
        b = fw.sbuf(st, name, shape, dt)
        q = q or (fw.pool if dt == BF16 else fw.sp)
        fw.dma(q, b[tuple(slice(None) for _ in shape)], src, writes=[b], key=b)
        return b

    def setup(self):
        nc, fw, pst = self.nc, self.fw, self.pst
        NS, depth = self.NS, self.depth
        cd = self.cd
        self.identF = self.load_const(pst, "identF", [128, 128], F32, cd["identF"][:, :])
        self.identB = self.load_const(pst, "identB", [128, 128], BF16, cd["identF"][:, :])
        self.TfB = self.load_const(pst, "TfB", [128, 128], BF16, cd["Tf"][:, :])
        self.TbB = self.load_const(pst, "TbB", [128, 128], BF16, cd["Tb"][:, :])
        self.IndB = self.load_const(pst, "IndB", [128, 4], BF16, cd["Ind"][:, :])
        self.maskA = self.load_const(pst, "maskA", [32, 256], F32, cd["maskA"][:, :])
        self.onesB = self.load_const(pst, "onesB", [128, 128], BF16, cd["ones"][:, :])
        self.onesF = self.load_const(pst, "onesF", [128, 128], F32, cd["ones"][:, :])
        self.NEG = self.load_const(pst, "NEG", [128, 64], F32, cd["NEG"][:, :])
        self.modT = fw.sbuf(pst, "modT", [128, depth * 72 * NS], F32)
        self.gT = fw.sbuf(pst, "gT", [128, depth * 24], F32)
        self.Acol = fw.sbuf(pst, "Acol", [128, depth * 3 * NS * 8], F32)
        self.gcol = fw.sbuf(pst, "gcol", [128, depth * 3 * NS * 8], F32)
        self.ghgT = fw.sbuf(pst, "ghgT", [128, depth], F32)
        self.gqrow = fw.sbuf(pst, "gqrow", [1, depth * 64], F32)
        self.gkrow = fw.sbuf(pst, "gkrow", [1, depth * 64], F32)

        with contextlib.ExitStack() as st:
            fw.begin_phase()
            ps = [fw.psum(st, f"sps{i}", [128, 512], F32) for i in range(3)]
            bmodT = fw.sbuf(st, "bmodT", [128, depth * 72], F32)
            scT = fw.sbuf(st, "scT", [128, NS * 8], F32)
            rows = fw.sbuf(st, "rows", [96, 128], F32)
            crow = fw.sbuf(st, "crow", [NS * 8, 128], F32)
            crow2 = fw.sbuf(st, "crow2", [NS * 8, 128], F32)
            hrow = fw.sbuf(st, "hrow", [depth, 128], F32)
            self.lbrow = fw.sbuf(st, "lbrow", [1, depth * 1024], F32)

            def tr_rows(src_ap, n, dst, dst_off, tmp):
                fw.dma(fw.sp, tmp[0:n, :], src_ap, writes=[tmp], key=tmp)
                fw.op(fw.pe, lambda: nc.tensor.transpose(ps[0][:, 0:n], tmp[0:n, :], self.identF[0:n, 0:n]),
                      reads=[tmp, self.identF], writes=[ps[0]])
                fw.op(fw.dve, lambda: nc.vector.tensor_copy(dst[:, dst_off:dst_off + n], ps[0][:, 0:n]),
                      reads=[ps[0]], writes=[dst])

            for r0 in range(0, depth * 72, 96):
                n = min(96, depth * 72 - r0)
                tr_rows(self.b_mod[r0:r0 + n, :], n, bmodT, r0, rows)
            for r0 in range(0, depth * 24, 96):
                n = min(96, depth * 24 - r0)
                tr_rows(self.norm_g[r0:r0 + n, :], n, self.gT, r0, rows)
            fw.dma(fw.sp, hrow[:, :], self.hgg[:, :], writes=[hrow], key=hrow)
            fw.op(fw.pe, lambda: nc.tensor.transpose(ps[1][:, 0:depth], hrow[:, :], self.identF[0:depth, 0:depth]),
                  reads=[hrow, self.identF], writes=[ps[1]])
            fw.op(fw.dve, lambda: nc.vector.tensor_scalar(self.ghgT[:, :], ps[1][:, 0:depth], 0.5, None, ALU.mult),
                  reads=[ps[1]], writes=[self.ghgT])
            fw.dma(fw.sp, crow[:, :], self.c[:, :], writes=[crow], key=crow)
            fw.op(fw.act, lambda: nc.scalar.activation(crow2[:, :], crow[:, :], AF.Silu), reads=[crow], writes=[crow2])
            fw.op(fw.pe, lambda: nc.tensor.transpose(ps[2][:, 0:NS * 8], crow2[:, :], self.identF[0:NS * 8, 0:NS * 8]),
                  reads=[crow2, self.identF], writes=[ps[2]])
            fw.op(fw.dve, lambda: nc.vector.tensor_copy(scT[:, :], ps[2][:, 0:NS * 8]), reads=[ps[2]], writes=[scT])
            scT3 = scT[:, :].rearrange("p (s k) -> p s k", s=NS)
            fw.dma(fw.sp, self.gqrow[:, :], self.gq[:, :], writes=[self.gqrow], key=self.gqrow)
            fw.dma(fw.sp, self.gkrow[:, :], self.gk[:, :], writes=[self.gkrow], key=self.gkrow)
            praw = fw.sbuf(st, "praw", [1, depth * 1024], F32)
            pr4 = praw[:, :].rearrange("o (l d k) -> o l d k", l=depth, d=2)
            fw.dma(fw.sp, pr4[:, :, 0, :], self.lbf[:, :].rearrange("o (l k) -> o l k", l=depth), writes=[praw], key=praw)
            fw.dma(fw.sp, pr4[:, :, 1, :], self.lbb[:, :].rearrange("o (l k) -> o l k", l=depth), writes=[praw], key=praw)
            mx = fw.sbuf(st, "mx", [1, 1024], F32)
            sm = fw.sbuf(st, "sm", [1, 1024], F32)
            pe_ = fw.sbuf(st, "pexp", [1, depth * 1024], F32)
            pl = lambda t, l: t[:, l * 1024:(l + 1) * 1024]
            fw.op(fw.dve, lambda: nc.vector.tensor_copy(mx[:, :], pl(praw, 0)), reads=[praw], writes=[mx])
            for l in range(1, depth):
                fw.op(fw.dve, lambda l=l: nc.vector.tensor_tensor(mx[:, :], mx[:, :], pl(praw, l), ALU.max),
                      reads=[praw, mx], writes=[mx])
            for l in range(depth):
                fw.op(fw.dve, lambda l=l: nc.vector.tensor_tensor(pl(pe_, l), pl(praw, l), mx[:, :], ALU.subtract),
                      reads=[praw, mx], writes=[pe_])
            fw.op(fw.act, lambda: nc.scalar.activation(pe_[:, :], pe_[:, :], AF.Exp), reads=[pe_], writes=[pe_])
            fw.op(fw.dve, lambda: nc.vector.tensor_copy(sm[:, :], pl(pe_, 0)), reads=[pe_], writes=[sm])
            for l in range(1, depth):
                fw.op(fw.dve, lambda l=l: nc.vector.tensor_tensor(sm[:, :], sm[:, :], pl(pe_, l), ALU.add),
                      reads=[pe_, sm], writes=[sm])
            fw.op(fw.dve, lambda: nc.vector.reciprocal(sm[:, :], sm[:, :]), reads=[sm], writes=[sm])
            fw.op(fw.dve, lambda: nc.vector.memset(pl(self.lbrow, 0), 0.0), writes=[self.lbrow])
            for l in range(1, depth):
                fw.op(fw.dve, lambda l=l: nc.vector.tensor_tensor(pl(pe_, l), pl(pe_, l), sm[:, :], ALU.mult),
                      reads=[pe_, sm], writes=[pe_])
                if l == 1:
                    fw.op(fw.dve, lambda l=l: nc.vector.tensor_copy(pl(self.lbrow, 1), pl(pe_, 1)),
                          reads=[pe_], writes=[self.lbrow])
                else:
                    fw.op(fw.dve, lambda l=l: nc.vector.tensor_tensor(pl(self.lbrow, l), pl(self.lbrow, l - 1), pl(pe_, l), ALU.add),
                          reads=[pe_, self.lbrow], writes=[self.lbrow])

            fw.dma(fw.sp, self.lbD[:, :], self.lbrow[:, :], reads=[self.lbrow], writes=[self.db("lbD")], key=self.lbrow)
            wm = [fw.sbuf(st, f"wm{i}", [128, 8, 1152], F32) for i in range(2)]
            mps = fw.psum(st, "mps", [128, 512], F32)
            modT4 = self.modT[:, :].rearrange("p (l j s) -> p l j s", l=depth, j=72)
            k = 0
            for l in range(depth):
                for pc in range(8):
                    w = wm[k % 2]
                    k += 1
                    fw.dma(fw.sp, w[:, :, :], self.w_mod[l, :, pc * 1152:(pc + 1) * 1152].rearrange("(c p) j -> p c j", p=128),
                           writes=[w], key=w)
                    for jj in range(9):
                        jc = pc * 9 + jj
                        for ck in range(8):
                            fw.op(fw.pe, lambda w=w, jj=jj, jc=jc, ck=ck: nc.tensor.matmul(
                                mps[:, jc * NS:(jc + 1) * NS], w[:, ck, jj * 128:(jj + 1) * 128], scT3[:, :, ck],
                                start=(ck == 0), stop=(ck == 7)), reads=[w, scT], writes=[mps])
                fw.op(fw.dve, lambda l=l: nc.vector.tensor_tensor(
                    modT4[:, l], mps[:, 0:72 * NS].rearrange("p (j s) -> p j s", s=NS),
                    bmodT[:, l * 72:(l + 1) * 72].unsqueeze(2).to_broadcast([128, 72, NS]), ALU.add),
                    reads=[mps, bmodT], writes=[self.modT])
            tmp8 = fw.sbuf(st, "tmp8", [128, 8], F32)
            A5 = self.Acol[:, :].rearrange("p (l u s c) -> p l u s c", l=depth, u=3, s=NS)
            g5 = self.gcol[:, :].rearrange("p (l u s c) -> p l u s c", l=depth, u=3, s=NS)
            for l in range(depth):
                for u in range(3):
                    for s in range(NS):
                        sc = modT4[:, l, u * 24 + 8:u * 24 + 16, s]
                        gt = modT4[:, l, u * 24 + 16:u * 24 + 24, s]
                        fw.op(fw.dve, lambda sc=sc: nc.vector.tensor_scalar(tmp8[:, :], sc, 1.0, None, ALU.add),
                              reads=[self.modT], writes=[tmp8])
                        fw.op(fw.dve, lambda l=l, u=u, s=s: nc.vector.tensor_tensor(
                            A5[:, l, u, s, :], tmp8[:, :], self.gT[:, (l * 3 + u) * 8:(l * 3 + u) * 8 + 8], ALU.mult),
                            reads=[tmp8, self.gT], writes=[self.Acol])
                        fac = 1.0 if u == 1 else 0.5
                        fw.op(fw.dve, lambda l=l, u=u, s=s, gt=gt, fac=fac: nc.vector.tensor_scalar(
                            g5[:, l, u, s, :], gt, fac, None, ALU.mult), reads=[self.modT], writes=[self.gcol])
            fw.end_phase()
        self.modT4 = self.modT[:, :].rearrange("p (l j s) -> p l j s", l=depth, j=72)
        self.A5 = self.Acol[:, :].rearrange("p (l u s c) -> p l u s c", l=depth, u=3, s=NS)
        self.g5 = self.gcol[:, :].rearrange("p (l u s c) -> p l u s c", l=depth, u=3, s=NS)

    def phase_in(self):
        nc, fw = self.nc, self.fw
        with contextlib.ExitStack() as st:
            fw.begin_phase()
            xin = [fw.sbuf(st, f"xin{i}", [128, D], F32) for i in range(3)]
            stg = [fw.sbuf(st, f"stg{i}", [128, 8, TT], F32) for i in range(2)]
            ps = [fw.psum(st, f"ips{i}", [128, 512], F32) for i in range(4)]
            nb = 0
            for g in range(self.T // TT):
                sg = stg[g % 2]
                for b in range(4):
                    xi = xin[nb % 3]
                    tok = g * TT + b * 128
                    fw.dma(fw.sp, xi[:, :], self.x[tok:tok + 128, :], writes=[xi], key=xi)
                    for hf in range(2):
                        p = ps[(nb * 2 + hf) % 4]
                        for cc in range(4):
                            c = hf * 4 + cc
                            fw.op(fw.pe, lambda p=p, cc=cc, c=c, xi=xi: nc.tensor.transpose(
                                p[:, cc * 128:(cc + 1) * 128], xi[:, c * 128:(c + 1) * 128], self.identF[:, :]),
                                reads=[xi, self.identF], writes=[p])
                        e = fw.act if hf == 0 else fw.dve
                        dst = sg[:, hf * 4:(hf + 1) * 4, b * 128:(b + 1) * 128]
                        src = p[:, :].rearrange("p (c t) -> p c t", c=4)
                        if hf == 0:
                            fw.op(e, lambda dst=dst, src=src: nc.scalar.copy(dst, src), reads=[p], writes=[sg])
                        else:
                            fw.op(e, lambda dst=dst, src=src: nc.vector.tensor_copy(dst, src), reads=[p], writes=[sg])
                    nb += 1
                fw.dma(fw.pool, self.xT[:, :, g * TT:(g + 1) * TT], sg[:, :, :], reads=[sg],
                       writes=[self.db("xT", g, c) for c in range(8)], key=sg)
            fw.end_phase()

    def phase_out(self):
        nc, fw = self.nc, self.fw
        with contextlib.ExitStack() as st:
            fw.begin_phase()
            xa = [fw.sbuf(st, f"oxa{i}", [128, 8, TT], F32) for i in range(2)]
            yo = [fw.sbuf(st, f"yo{i}", [128, D], F32) for i in range(3)]
            ps = [fw.psum(st, f"ops{i}", [128, 512], F32) for i in range(4)]
            nb = 0
            for g in range(self.T // TT):
                xg = xa[g % 2]
                fw.dma(fw.sp, xg[:, :, :], self.xT[:, :, g * TT:(g + 1) * TT],
                       reads=[self.db("xT", g, c) for c in range(8)], writes=[xg], key=xg)
                for b in range(4):
                    yb = yo[nb % 3]
                    tok = g * TT + b * 128
                    for hf in range(2):
                        p = ps[(nb * 2 + hf) % 4]
                        for cc in range(4):
                            c = hf * 4 + cc
                            fw.op(fw.pe, lambda p=p, cc=cc, c=c, xg=xg, b=b: nc.tensor.transpose(
                                p[:, cc * 128:(cc + 1) * 128], xg[:, c, b * 128:(b + 1) * 128], self.identF[:, :]),
                                reads=[xg, self.identF], writes=[p])
                        if hf == 0:
                            fw.op(fw.act, lambda p=p, yb=yb: nc.scalar.copy(yb[:, 0:512], p[:, :]), reads=[p], writes=[yb])
                        else:
                            fw.op(fw.dve, lambda p=p, yb=yb: nc.vector.tensor_copy(yb[:, 512:1024], p[:, :]), reads=[p], writes=[yb])
                    fw.dma(fw.pool, self.y[tok:tok + 128, :], yb[:, :], reads=[yb], writes=[self.db("y", tok)], key=yb)
                    nb += 1
            fw.end_phase()

    def prep(self, g, l, sub, B, mode, stats_only=False):
        nc, fw = self.nc, self.fw
        s = self.seq_of(g * TT)
        xa, hT, msp = B["xa"], B["hT"], B["msp"]
        fw.dma(fw.sp, xa[:, :, :], self.xT[:, :, g * TT:(g + 1) * TT],
               reads=[self.db("xT", g, c) for c in range(8)], writes=[xa], key=xa)
        for c in range(8):
            sq = B["sq"][c % len(B["sq"])]
            fw.op(fw.act, lambda c=c, sq=sq: nc.scalar.activation(sq[:, :], xa[:, c, :], AF.Square), reads=[xa], writes=[sq])
            fw.op(fw.pe, lambda c=c, sq=sq: nc.tensor.matmul(msp[:, :], self.onesB[:, :], sq[:, :], start=(c == 0), stop=(c == 7)),
                  reads=[sq, self.onesB], writes=[msp])
        rstd = B["rstd"]
        if mode == "sqrt":
            fw.op(fw.act, lambda: nc.scalar.activation(rstd[:, :], msp[:, :], AF.Sqrt, bias=B["epsc"][:, 0:1], scale=1.0 / D),
                  reads=[msp, B["epsc"]], writes=[rstd])
            fw.op(fw.dve, lambda: nc.vector.reciprocal(rstd[:, :], rstd[:, :]), reads=[rstd], writes=[rstd])
        else:
            fw.op(fw.act, lambda: nc.scalar.activation(rstd[:, :], msp[:, :], AF.Ln, bias=B["epsc"][:, 0:1], scale=1.0 / D),
                  reads=[msp, B["epsc"]], writes=[rstd])
            fw.op(fw.act, lambda: nc.scalar.activation(rstd[:, :], rstd[:, :], AF.Exp, scale=-0.5), reads=[rstd], writes=[rstd])
        if not stats_only:
            self.prep_mod(g, l, sub, B, mode, range(8))

    def prep_mod(self, g, l, sub, B, mode, chunks):
        nc, fw = self.nc, self.fw
        s = self.seq_of(g * TT)
        xa, hT, rstd = B["xa"], B["hT"], B["rstd"]
        for c in chunks:
            t = B["tmp"][c % len(B["tmp"])]
            if mode == "sqrt":
                fw.op(fw.dve, lambda c=c, t=t: nc.vector.tensor_tensor(t[:, :], xa[:, c, :], rstd[:, :], ALU.mult),
                      reads=[xa, rstd], writes=[t])
                fw.op(fw.act, lambda c=c, t=t: nc.scalar.activation(
                    hT[:, c, :], t[:, :], AF.Identity, bias=self.modT4[:, l, sub * 24 + c, s:s + 1],
                    scale=self.A5[:, l, sub, s, c:c + 1]), reads=[t, self.modT, self.Acol], writes=[hT])
            else:
                fw.op(fw.dve, lambda c=c, t=t: nc.vector.scalar_tensor_tensor(
                    t[:, :], xa[:, c, :], self.A5[:, l, sub, s, c:c + 1], rstd[:, :], ALU.mult, ALU.mult),
                    reads=[xa, rstd, self.Acol], writes=[t])
                fw.op(fw.dve, lambda c=c, t=t: nc.vector.tensor_scalar(
                    hT[:, c, :], t[:, :], self.modT4[:, l, sub * 24 + c, s:s + 1], None, ALU.add),
                    reads=[t, self.modT], writes=[hT])

    def epsc(self, st):
        fw, nc = self.fw, self.nc
        e = fw.sbuf(st, "epsc", [128, 1], F32)
        fw.op(fw.dve, lambda: nc.vector.memset(e[:, :], EPS), writes=[e])
        return e

    GU_PIECES = ((0, 4), (4, 7), (11, 11))

    @staticmethod
    def gu_piece(f):
        return (0, f) if f < 4 else ((1, f - 4) if f < 11 else (2, f - 11))

    def ffn_weights(self, st, l, which):
        fw = self.fw
        Wg = [fw.sbuf(st, f"Wg{i}", [128, 8, n * 128], BF16) for i, (f0, n) in enumerate(self.GU_PIECES)]
        Wu = [fw.sbuf(st, f"Wu{i}", [128, 8, n * 128], BF16) for i, (f0, n) in enumerate(self.GU_PIECES)]
        Wd = [fw.sbuf(st, f"Wd{i}", [128, NF // 2, D], BF16) for i in range(2)]
        for i, (f0, n) in enumerate(self.GU_PIECES):
            for wt, src in ((Wg, self.wg), (Wu, self.wu)):
                fw.dma(fw.pool, wt[i][:, :, :],
                       src[l, which, :, f0 * 128:(f0 + n) * 128].rearrange("(c p) f -> p c f", p=128),
                       writes=[wt[i]], key=wt[i])
        for hf in range(2):
            fw.dma(fw.pool, Wd[hf][:, :, :],
                   self.wd[l, which, hf * 11 * 128:(hf + 1) * 11 * 128, :].rearrange("(c p) d -> p c d", p=128),
                   writes=[Wd[hf]], key=Wd[hf])
        return Wg, Wu, Wd

    def ffn(self, l, sub, W=None):
        nc, fw = self.nc, self.fw
        which = 0 if sub == 0 else 1
        NG = self.T // TT
        with contextlib.ExitStack() as st:
            fw.begin_phase()
            Wg, Wu, Wd = W if W is not None else self.ffn_weights(st, l, which)
            B = dict(
                xa=fw.sbuf(st, "xa", [128, 8, TT], F32),
                hT=fw.sbuf(st, "hT", [128, 8, TT], BF16),
                msp=fw.psum(st, "msp", [128, 512], F32),
                sq=[fw.sbuf(st, f"sq{i}", [128, TT], BF16) for i in range(2)],
                tmp=[fw.sbuf(st, f"tmp{i}", [128, TT], F32) for i in range(2)],
                rstd=fw.sbuf(st, "rstd", [128, TT], F32),
                epsc=self.epsc(st),
            )
            hT = B["hT"]
            aT = fw.sbuf(st, "aT", [128, NF, TT], BF16)
            sgb = [fw.sbuf(st, f"sg{i}", [128, TT], BF16) for i in range(2)]
            xr = [fw.sbuf(st, f"xr{i}", [128, TT], F32) for i in range(3)]
            gps = [fw.psum(st, f"gps{i}", [128, 512], F32) for i in range(2)]
            ups = [fw.psum(st, f"ups{i}", [128, 512], F32) for i in range(2)]
            dps = [fw.psum(st, f"dps{i}", [128, 512], F32) for i in range(2)]

            self.prep(0, l, sub, B, "sqrt")
            nxr = 0
            for g in range(NG):
                s = self.seq_of(g * TT)
                for f in range(NF):
                    gp, up = gps[f % 2], ups[f % 2]
                    pi, pf = self.gu_piece(f)
                    for ck in range(8):
                        fw.op(fw.pe, lambda f=f, ck=ck, gp=gp: nc.tensor.matmul(
                            gp[:, :], Wg[pi][:, ck, pf * 128:(pf + 1) * 128], hT[:, ck, :], start=(ck == 0), stop=(ck == 7)),
                            reads=[Wg[pi], hT], writes=[gp])
                    for ck in range(8):
                        fw.op(fw.pe, lambda f=f, ck=ck, up=up: nc.tensor.matmul(
                            up[:, :], Wu[pi][:, ck, pf * 128:(pf + 1) * 128], hT[:, ck, :], start=(ck == 0), stop=(ck == 7)),
                            reads=[Wu[pi], hT], writes=[up])
                    sg = sgb[f % 2]
                    fw.op(fw.act, lambda gp=gp, sg=sg: nc.scalar.activation(sg[:, :], gp[:, :], AF.Silu), reads=[gp], writes=[sg])
                    fw.op(fw.dve, lambda f=f, up=up, sg=sg: nc.vector.tensor_tensor(aT[:, f, :], sg[:, :], up[:, :], ALU.mult),
                          reads=[sg, up], writes=[aT])
                if g + 1 < NG:
                    self.prep(g + 1, l, sub, B, "sqrt")
                for c in range(8):
                    dp = dps[c % 2]
                    x_ = xr[nxr % 3]
                    nxr += 1
                    xb = self.db("xT", g, c)
                    fw.dma(fw.sp, x_[:, :], self.xT[:, c, g * TT:(g + 1) * TT], reads=[xb], writes=[x_], key=x_)
                    for f in range(NF):
                        fw.op(fw.pe, lambda f=f, c=c, dp=dp: nc.tensor.matmul(
                            dp[:, :], Wd[f // 11][:, f % 11, c * 128:(c + 1) * 128], aT[:, f, :], start=(f == 0), stop=(f == NF - 1)),
                            reads=[Wd[f // 11], aT], writes=[dp])
                    fw.op(fw.dve, lambda c=c, dp=dp, x_=x_, s=s: nc.vector.scalar_tensor_tensor(
                        x_[:, :], dp[:, :], self.g5[:, l, sub, s, c:c + 1], x_[:, :], ALU.mult, ALU.add),
                        reads=[dp, x_, self.gcol], writes=[x_])
                    fw.dma(fw.pool, self.xT[:, c, g * TT:(g + 1) * TT], x_[:, :], reads=[x_], writes=[xb], key=x_)
            fw.end_phase()

    def mixer(self, l, ph="ABCD"):
        if "A" in ph:
            self.mix_A(l)
        if "B" in ph:
            self.mix_B(l)
        if "C" in ph:
            self.mix_C(l)
        if "D" in ph:
            self.mix_D(l)

    def bcast_row(self, ps, row_ap, n, dst_ap_fn, rd):
        nc, fw = self.nc, self.fw
        fw.op(fw.pe, lambda: nc.tensor.matmul(ps[:, 0:n], self.onesF[0:1, :], row_ap, start=True, stop=True),
              reads=[self.onesF] + list(rd), writes=[ps])
        dst_ap_fn(ps)

    def mix_A(self, l):
        nc, fw = self.nc, self.fw
        NG = self.T // TT
        with contextlib.ExitStack() as st:
            fw.begin_phase()
            Winq = [fw.sbuf(st, f"Win{q4}", [128, 8, 1024], BF16) for q4 in range(4)]
            for q4 in range(4):
                fw.dma(fw.pool, Winq[q4][:, :, :],
                       self.w_in[l, :, q4 * 1024:(q4 + 1) * 1024].rearrange("(c p) f -> p c f", p=128), writes=[Winq[q4]], key=Winq[q4])
            B = dict(
                xa=fw.sbuf(st, "xa", [128, 8, TT], F32),
                hT=fw.sbuf(st, "hT", [128, 8, TT], BF16),
                msp=fw.psum(st, "msp", [128, 512], F32),
                sq=[fw.sbuf(st, f"sq{i}", [128, TT], BF16) for i in range(2)],
                tmp=[fw.sbuf(st, f"tmp{i}", [128, TT], F32) for i in range(1)],
                rstd=fw.sbuf(st, "rstd", [128, TT], F32),
                epsc=self.epsc(st),
            )
            hT, msp = B["hT"], B["msp"]
            pj = [fw.psum(st, f"pj{i}", [128, 512], F32) for i in range(3)]
            bp = [fw.psum(st, f"bp{i}", [128, 512], F32) for i in range(2)]
            trp = [fw.psum(st, f"trp{i}", [128, 1024], BF16) for i in range(2)]
            c0B = fw.sbuf(st, "c0B", [128, 1024], F32)
            c1B = fw.sbuf(st, "c1B", [128, 1024], F32)
            gqB = fw.sbuf(st, "gqB", [128, 64], F32)
            gkB = fw.sbuf(st, "gkB", [128, 64], F32)
            with contextlib.ExitStack() as st2:
                lbr = fw.sbuf(st2, "lbr", [1, 1024], F32)
                fw.dma(fw.sp, lbr[:, :], self.lbD[0:1, l * 1024:(l + 1) * 1024], reads=[self.db("lbD")], writes=[lbr], key=lbr)
                for hf in range(2):
                    def wr(ps, hf=hf):
                        fw.op(fw.dve, lambda: nc.vector.tensor_scalar(c0B[:, hf * 512:(hf + 1) * 512], ps[:, :], 0.5, 0.5, ALU.mult, ALU.add),
                              reads=[ps], writes=[c0B])
                        fw.op(fw.dve, lambda: nc.vector.tensor_scalar(c1B[:, hf * 512:(hf + 1) * 512], ps[:, :], -0.5, 0.5, ALU.mult, ALU.add),
                              reads=[ps], writes=[c1B])
                    self.bcast_row(pj[hf], lbr[0:1, hf * 512:(hf + 1) * 512], 512, wr, [lbr])
                fw.barrier()
            for row, dstB, fac in ((self.gqrow, gqB, 0.125), (self.gkrow, gkB, 1.0)):
                def wr(ps, dstB=dstB, fac=fac):
                    fw.op(fw.dve, lambda: nc.vector.tensor_scalar(dstB[:, :], ps[:, 0:64], fac, None, ALU.mult),
                          reads=[ps], writes=[dstB])
                self.bcast_row(pj[2], row[0:1, l * 64:(l + 1) * 64], 64, wr, [row])
            fw.op(fw.dve, lambda: nc.vector.tensor_tensor(gkB[:, :], gkB[:, :], gqB[:, :], ALU.mult), reads=[gkB, gqB], writes=[gkB])

            def stage(name, shape, dt, n=1):
                t = [fw.sbuf(st, f"{name}{i}", shape, dt) for i in range(n)]
                return t * (2 // n)
            S_qtT = stage("S_qtT", [128, 2, 4, TT], BF16)
            S_ktT = stage("S_ktT", [128, 2, 4, TT], BF16)
            S_sgT = stage("S_sgT", [128, 4, TT], BF16)
            S_QT = stage("S_QT", [64, 8, TT], BF16)
            S_KT = stage("S_KT", [64, 8, TT], BF16)
            qs = stage("qs", [128, 512], BF16, 2)
            th = stage("th", [128, 1024], F32)
            ff = th
            lfh = stage("lfh", [128, 1024], BF16, 2)
            lfl = stage("lfl", [128, 1024], BF16, 2)
            kk = stage("kk", [128, 1024], BF16, 2)
            E = stage("E", [128, 1024], F32)
            Ei = stage("Ei", [128, 1024], F32)
            qt = stage("qt", [128, 1024], BF16)
            kt = stage("kt", [128, 1024], BF16, 2)
            vt = stage("vt", [128, 512], BF16, 2)
            sgt = stage("sgt", [128, 512], BF16, 2)
            thg = stage("thg", [128, 512], F32)
            nsq = stage("nsq", [128, 1024], F32)
            nss = stage("nss", [128, 16], F32)
            nn = nsq
            nqk = stage("nqk", [128, 1024], BF16, 2)
            nv = stage("nv", [128, 512], BF16, 2)
            dec = stage("dec", [128, 32], F32, 2)

            hTs = [B["hT"], fw.sbuf(st, "hT2", [128, 8, TT], BF16)]
            npj = [0]

            def head(g, b, nb):
                i2 = nb % 2
                hT = hTs[g % 2]
                tok = g * TT + b * 128

                def proj(cg):
                    p = pj[npj[0] % 3]
                    npj[0] += 1
                    for ck in range(8):
                        fw.op(fw.pe, lambda ck=ck: nc.tensor.matmul(
                            p[:, :], hT[:, ck, b * 128:(b + 1) * 128], Winq[cg // 2][:, ck, (cg % 2) * 512:(cg % 2 + 1) * 512],
                            start=(ck == 0), stop=(ck == 7)), reads=[hT, Winq[cg // 2]], writes=[p])
                    return p
                for dr_ in range(2):
                    p = proj(1 + dr_)
                    fw.op(fw.act, lambda p=p, dr_=dr_: nc.scalar.activation(th[0][:, dr_ * 512:(dr_ + 1) * 512], p[:, :], AF.Tanh, scale=0.5),
                          reads=[p], writes=[th[0]])
                pq = proj(0)
                fw.op(fw.act, lambda: nc.scalar.activation(thg[0][:, :], pq[:, :], AF.Tanh, scale=0.5), reads=[pq], writes=[thg[0]])
                fw.op(fw.dve, lambda: nc.vector.scalar_tensor_tensor(qs[i2][:, :], thg[0][:, :], 1.0, pq[:, :], ALU.add, ALU.mult),
                      reads=[thg[0], pq], writes=[qs[i2]])
                pg = proj(4)
                fw.op(fw.act, lambda: nc.scalar.activation(thg[0][:, :], pg[:, :], AF.Tanh, scale=0.5), reads=[pg], writes=[thg[0]])
                fw.op(fw.dve, lambda: nc.vector.scalar_tensor_tensor(sgt[i2][:, :], thg[0][:, :], 1.0, pg[:, :], ALU.add, ALU.mult),
                      reads=[thg[0], pg], writes=[sgt[i2]])
                fw.op(fw.dve, lambda: nc.vector.tensor_tensor(th[0][:, :], th[0][:, :], c1B[:, :], ALU.mult),
                      reads=[th[0], c1B], writes=[th[0]])
                fw.op(fw.dve, lambda: nc.vector.tensor_tensor(th[0][:, :], th[0][:, :], c0B[:, :], ALU.add),
                      reads=[th[0], c0B], writes=[th[0]])
                fw.op(fw.pool, lambda: nc.gpsimd.tensor_scalar(kk[i2][:, :], th[0][:, :], -1.0, 1.0, ALU.mult, ALU.add),
                      reads=[th[0]], writes=[kk[i2]])
                fw.op(fw.act, lambda: nc.scalar.activation(th[0][:, :], th[0][:, :], AF.Ln), reads=[th[0]], writes=[th[0]])
                fw.op(fw.dve, lambda: nc.vector.tensor_copy(lfh[i2][:, :], th[0][:, :]), reads=[th[0]], writes=[lfh[i2]])
                fw.op(fw.dve, lambda: nc.vector.tensor_tensor(lfl[i2][:, :], th[0][:, :], lfh[i2][:, :], ALU.subtract),
                      reads=[th[0], lfh[i2]], writes=[lfl[i2]])
                pv = proj(3)
                fw.op(fw.act, lambda: nc.scalar.copy(vt[i2][:, :], pv[:, :]), reads=[pv], writes=[vt[i2]])
                fw.dma(fw.pool, self.vtok[tok:tok + 128, :], vt[i2][:, :], reads=[vt[i2]], writes=[self.db("vtok", g)], key=vt[i2])

            def head2(g, b, nb):
                i2 = nb % 2
                hT = hTs[g % 2]
                tok = g * TT + b * 128

                def proj(cg):
                    p = pj[npj[0] % 3]
                    npj[0] += 1
                    for ck in range(8):
                        fw.op(fw.pe, lambda ck=ck: nc.tensor.matmul(
                            p[:, :], hT[:, ck, b * 128:(b + 1) * 128], Winq[cg // 2][:, ck, (cg % 2) * 512:(cg % 2 + 1) * 512],
                            start=(ck == 0), stop=(ck == 7)), reads=[hT, Winq[cg // 2]], writes=[p])
                    return p
                pq_ = proj(5)
                fw.op(fw.act, lambda: nc.scalar.activation(nsq[0][:, 0:512], pq_[:, :], AF.Square), reads=[pq_], writes=[nsq[0]])
                pk_ = proj(6)
                fw.op(fw.act, lambda: nc.scalar.activation(nsq[0][:, 512:1024], pk_[:, :], AF.Square), reads=[pk_], writes=[nsq[0]])
                fw.op(fw.dve, lambda: nc.vector.tensor_reduce(
                    nss[0][:, 0:16], nsq[0][:, :].rearrange("p (h d) -> p h d", h=16), AX.X, ALU.add), reads=[nsq[0]], writes=[nss[0]])
                fw.op(fw.act, lambda: nc.scalar.activation(nss[0][:, 0:16], nss[0][:, 0:16], AF.Ln, bias=B["epsc"][:, 0:1], scale=1.0 / 64),
                      reads=[nss[0], B["epsc"]], writes=[nss[0]])
                fw.op(fw.act, lambda: nc.scalar.activation(nss[0][:, 0:16], nss[0][:, 0:16], AF.Exp, scale=-0.5), reads=[nss[0]], writes=[nss[0]])
                fw.op(fw.dve, lambda: nc.vector.tensor_tensor(
                    nqk[i2][:, 0:512].rearrange("p (h d) -> p h d", h=8), pq_[:, :].rearrange("p (h d) -> p h d", h=8),
                    nss[0][:, 0:8].unsqueeze(2).to_broadcast([128, 8, 64]), ALU.mult), reads=[pq_, nss[0]], writes=[nqk[i2]])
                fw.op(fw.dve, lambda: nc.vector.tensor_tensor(
                    nsq[0][:, 512:1024].rearrange("p (h d) -> p h d", h=8), pk_[:, :].rearrange("p (h d) -> p h d", h=8),
                    nss[0][:, 8:16].unsqueeze(2).to_broadcast([128, 8, 64]), ALU.mult), reads=[pk_, nss[0]], writes=[nsq[0]])
                fw.op(fw.pool, lambda: nc.gpsimd.tensor_tensor(
                    nqk[i2][:, 512:1024].rearrange("p (h d) -> p h d", h=8),
                    nsq[0][:, 512:1024].rearrange("p (h d) -> p h d", h=8),
                    gkB[:, :].unsqueeze(1).to_broadcast([128, 8, 64]), ALU.mult),
                    reads=[nsq[0], gkB], writes=[nqk[i2]])

            def head2b(g, b, nb):
                i2 = nb % 2
                hT = hTs[g % 2]
                tok = g * TT + b * 128
                p = pj[npj[0] % 3]
                npj[0] += 1
                for ck in range(8):
                    fw.op(fw.pe, lambda ck=ck: nc.tensor.matmul(
                        p[:, :], hT[:, ck, b * 128:(b + 1) * 128], Winq[3][:, ck, 512:1024],
                        start=(ck == 0), stop=(ck == 7)), reads=[hT, Winq[3]], writes=[p])
                fw.op(fw.act, lambda: nc.scalar.copy(nv[i2][:, :], p[:, :]), reads=[p], writes=[nv[i2]])
                fw.dma(fw.pool, self.Vna[tok:tok + 128, :], nv[i2][:, :], reads=[nv[i2]], writes=[self.db("Vna", g)], key=nv[i2])

            def tail(g, b, nb):
                i2 = nb % 2
                gi = 0
                tok = g * TT + b * 128
                blk = tok // 128
                for dr_ in range(2):
                    Tm = self.TfB if dr_ == 0 else self.TbB
                    fw.op(fw.pe, lambda dr_=dr_, Tm=Tm: nc.tensor.matmul(
                        bp[dr_][:, :], Tm[:, :], lfh[i2][:, dr_ * 512:(dr_ + 1) * 512], start=True, stop=False),
                        reads=[Tm, lfh[i2]], writes=[bp[dr_]])
                    fw.op(fw.pe, lambda dr_=dr_, Tm=Tm: nc.tensor.matmul(
                        bp[dr_][:, :], Tm[:, :], lfl[i2][:, dr_ * 512:(dr_ + 1) * 512], start=False, stop=True),
                        reads=[Tm, lfl[i2]], writes=[bp[dr_]])
                    fw.op(fw.act, lambda dr_=dr_: nc.scalar.activation(E[0][:, dr_ * 512:(dr_ + 1) * 512], bp[dr_][:, :], AF.Exp),
                          reads=[bp[dr_]], writes=[E[0]])
                    fw.op(fw.act, lambda dr_=dr_: nc.scalar.activation(Ei[0][:, dr_ * 512:(dr_ + 1) * 512], bp[dr_][:, :], AF.Exp, scale=-1.0),
                          reads=[bp[dr_]], writes=[Ei[0]])
                    fw.op(fw.dve, lambda dr_=dr_: nc.vector.scalar_tensor_tensor(
                        qt[0][:, dr_ * 512:(dr_ + 1) * 512], qs[i2][:, :], 0.5, E[0][:, dr_ * 512:(dr_ + 1) * 512], ALU.mult, ALU.mult),
                        reads=[qs[i2], E[0]], writes=[qt[0]])
                fw.op(fw.dve, lambda: nc.vector.tensor_tensor(kt[i2][:, :], kk[i2][:, :], Ei[0][:, :], ALU.mult),
                      reads=[kk[i2], Ei[0]], writes=[kt[i2]])
                for hd in range(8):
                    fw.op(fw.pe, lambda hd=hd: nc.tensor.matmul(
                        msp[:, hd * 4:(hd + 1) * 4], lfh[i2][:, hd * 128:(hd + 1) * 128], self.IndB[:, :], start=True, stop=False),
                        reads=[lfh[i2], self.IndB], writes=[msp])
                    fw.op(fw.pe, lambda hd=hd: nc.tensor.matmul(
                        msp[:, hd * 4:(hd + 1) * 4], lfl[i2][:, hd * 128:(hd + 1) * 128], self.IndB[:, :], start=False, stop=True),
                        reads=[lfl[i2], self.IndB], writes=[msp])
                fw.op(fw.act, lambda: nc.scalar.activation(dec[i2][:, :], msp[:, 0:32], AF.Exp), reads=[msp], writes=[dec[i2]])
                fw.dma(fw.pool, self.decT[blk, :, :], dec[i2][:, :], reads=[dec[i2]], writes=[self.db("decT", g)], key=dec[i2])
                for dr_ in range(2):
                    fw.dma(fw.pool, self.ktok[dr_, tok:tok + 128, :], kt[i2][:, dr_ * 512:(dr_ + 1) * 512], reads=[kt[i2]],
                           writes=[self.db("ktok", g)], key=kt[i2])

            def tail2(g, b, nb):
                i2 = nb % 2
                gi = 0
                tok = g * TT + b * 128
                for src, dstS, tp in ((qt[0], S_qtT, trp[0]), (kt[i2], S_ktT, trp[1])):
                    for hd in range(8):
                        fw.op(fw.pe, lambda hd=hd, src=src, tp=tp: nc.tensor.transpose(
                            tp[:, hd * 128:(hd + 1) * 128], src[:, hd * 128:(hd + 1) * 128], self.identB[:, :]),
                            reads=[src, self.identB], writes=[tp])
                    dsta = dstS[gi][:, :, :, b * 128:(b + 1) * 128]
                    srca = tp[:, :].rearrange("p (d h t) -> p d h t", d=2, h=4)
                    if src is qt[0]:
                        fw.op(fw.dve, lambda dsta=dsta, srca=srca: nc.vector.tensor_copy(dsta, srca), reads=[tp], writes=[dstS[gi]])
                    else:
                        fw.op(fw.act, lambda dsta=dsta, srca=srca: nc.scalar.copy(dsta, srca), reads=[tp], writes=[dstS[gi]])

            def tail2b(g, b, nb):
                i2 = nb % 2
                gi = 0
                bpb = bp[0][:, :].bitcast(BF16)
                for j in range(4):
                    fw.op(fw.pe, lambda j=j: nc.tensor.transpose(bpb[:, j * 128:(j + 1) * 128], sgt[i2][:, j * 128:(j + 1) * 128], self.identB[:, :]),
                          reads=[sgt[i2], self.identB], writes=[bp[0]])
                fw.op(fw.dve, lambda: nc.vector.tensor_copy(S_sgT[gi][:, :, b * 128:(b + 1) * 128],
                                                            bpb[:, 0:512].rearrange("p (h t) -> p h t", h=4)),
                      reads=[bp[0]], writes=[S_sgT[gi]])
                for w_, tp, dstS in ((0, trp[1], S_QT), (1, trp[0], S_KT)):
                    for h in range(8):
                        fw.op(fw.pe, lambda h=h, w_=w_, tp=tp: nc.tensor.transpose(
                            tp[0:64, h * 128:(h + 1) * 128], nqk[i2][:, w_ * 512 + h * 64:w_ * 512 + (h + 1) * 64], self.identB[:, :]),
                            reads=[nqk[i2], self.identB], writes=[tp])
                    if w_ == 0:
                        fw.op(fw.act, lambda tp=tp, dstS=dstS: nc.scalar.copy(
                            dstS[gi][:, :, b * 128:(b + 1) * 128], tp[0:64, :].rearrange("p (h t) -> p h t", h=8)),
                            reads=[tp], writes=[dstS[gi]])
                    else:
                        fw.op(fw.dve, lambda tp=tp, dstS=dstS: nc.vector.tensor_copy(
                            dstS[gi][:, :, b * 128:(b + 1) * 128], tp[0:64, :].rearrange("p (h t) -> p h t", h=8)),
                            reads=[tp], writes=[dstS[gi]])
                if b == 3:
                    tsl = slice(g * TT, (g + 1) * TT)
                    for dr_ in range(2):
                        fw.dma(fw.pool, self.qtT[dr_, :, :, tsl], S_qtT[gi][:, dr_, :, :], reads=[S_qtT[gi]], writes=[self.db("qtT", g)], key=S_qtT[gi])
                        fw.dma(fw.pool, self.ktT[dr_, :, :, tsl], S_ktT[gi][:, dr_, :, :], reads=[S_ktT[gi]], writes=[self.db("ktT", g)], key=S_ktT[gi])
                    fw.dma(fw.pool, self.sgT[:, :, tsl], S_sgT[gi][:, :, :], reads=[S_sgT[gi]], writes=[self.db("sgT", g)], key=S_sgT[gi])
                    fw.dma(fw.pool, self.QT[:, :, tsl], S_QT[gi][:, :, :], reads=[S_QT[gi]], writes=[self.db("QT", g)], key=S_QT[gi])
                    fw.dma(fw.pool, self.KT[:, :, tsl], S_KT[gi][:, :, :], reads=[S_KT[gi]], writes=[self.db("KT", g)], key=S_KT[gi])

            seqb = [(g, b) for g in range(NG) for b in range(4)]
            B["hT"] = hTs[0]
            self.prep(0, l, 1, B, "lnexp")
            def prep_piece(g2, b2):
                if g2 + 1 >= NG:
                    return
                if b2 == 0:
                    B["hT"] = hTs[(g2 + 1) % 2]
                    self.prep(g2 + 1, l, 1, B, "lnexp", stats_only=True)
                self.prep_mod(g2 + 1, l, 1, B, "lnexp", [2 * b2, 2 * b2 + 1])

            head(seqb[0][0], seqb[0][1], 0)
            head2(seqb[0][0], seqb[0][1], 0)
            head2b(seqb[0][0], seqb[0][1], 0)
            prep_piece(0, 0)
            for i, (g, b) in enumerate(seqb):
                tail(g, b, i)
                if i + 1 < len(seqb):
                    g2, b2 = seqb[i + 1]
                    head(g2, b2, i + 1)
                tail2(g, b, i)
                if i + 1 < len(seqb):
                    head2(g2, b2, i + 1)
                tail2b(g, b, i)
                if i + 1 < len(seqb):
                    head2b(g2, b2, i + 1)
                    prep_piece(g2, b2)
            fw.end_phase()

    def mix_B(self, l):
        nc, fw = self.nc, self.fw
        NS = self.NS
        groups, cur, tot = [], [], 0
        for s in range(NS):
            if cur and tot + self.seqs[s] > 4096:
                groups.append(cur)
                cur, tot = [], 0
            cur.append(s)
            tot += self.seqs[s]
        groups.append(cur)
        for grp in groups:
            self.mix_B_group(l, grp)

    def mix_B_group(self, l, grp):
        nc, fw = self.nc, self.fw
        NS = self.NS
        with contextlib.ExitStack() as st:
            fw.begin_phase()
            epsc = self.epsc(st)
            chains = []
            for s in grp:
                for dr_ in range(2):
                    ch = dict(
                        s=s, dr=dr_,
                        S=fw.sbuf(st, f"S{s}{dr_}", [128, 512], F32),
                        Sb=fw.sbuf(st, f"Sb{s}{dr_}", [128, 512], BF16),
                        qT=[fw.sbuf(st, f"bq{s}{dr_}{i}", [128, 4, 128], BF16) for i in range(2)],
                        kT=[fw.sbuf(st, f"bk{s}{dr_}{i}", [128, 4, 128], BF16) for i in range(2)],
                        kt=[fw.sbuf(st, f"bkt{s}{dr_}{i}", [32, 4, 512], BF16) for i in range(2)],
                        v=[fw.sbuf(st, f"bv{s}{dr_}{i}", [32, 4, 512], BF16) for i in range(2)],
                        dc=[fw.sbuf(st, f"bd{s}{dr_}{i}", [128, 32], F32) for i in range(2)],
                        A=[fw.sbuf(st, f"bA{s}{dr_}{i}", [32, 128], BF16) for i in range(2)],
                        tmp=fw.sbuf(st, f"bt{s}{dr_}", [128, 512], F32),
                    )
                    fw.op(fw.dve, lambda ch=ch: nc.vector.memset(ch["S"][:, :], 0.0), writes=[ch["S"]])
                    fw.op(fw.pool, lambda ch=ch: nc.gpsimd.memset(ch["Sb"][:, :], 0.0), writes=[ch["Sb"]])
                    chains.append(ch)
            ops = [fw.psum(st, f"bo{i}", [128, 512], F32) for i in range(4)]
            aps = [fw.psum(st, f"ba{i}", [128, 512], F32) for i in range(2)]
            sps = [fw.psum(st, f"bs{i}", [128, 512], F32) for i in range(2)]
            oacc = {s: fw.sbuf(st, f"oacc{s}", [128, 4, self.seqs[s]], F32) for s in grp}
            maxnb = max(self.seqs[s] for s in grp) // 128
            cnt = dict(a=0, s=0)
            nch = len(chains)
            for step in range(maxnb):
                act = []
                for j, ch in enumerate(chains):
                    s, dr_ = ch["s"], ch["dr"]
                    nbk = self.seqs[s] // 128
                    if step >= nbk:
                        continue
                    bi = step if dr_ == 0 else nbk - 1 - step
                    tok = self.seq_off[s] + bi * 128
                    g = tok // TT
                    i2 = step % 2
                    ch["cur"] = dict(qT=ch["qT"][i2], kT=ch["kT"][i2], kt=ch["kt"][i2], v=ch["v"][i2], dc=ch["dc"][i2],
                                     bi=bi, nbk=nbk, op=ops[(j + nch * (step % 2)) % 4 if 2 * nch <= 4 else j])
                    c_ = ch["cur"]
                    fw.dma(fw.sp, c_["qT"][:, :, :], self.qtT[dr_, :, :, tok:tok + 128], reads=[self.db("qtT", g)], writes=[c_["qT"]], key=c_["qT"])
                    fw.dma(fw.sp, c_["kT"][:, :, :], self.ktT[dr_, :, :, tok:tok + 128], reads=[self.db("ktT", g)], writes=[c_["kT"]], key=c_["kT"])
                    fw.dma(fw.sp, c_["kt"][:, :, :], self.ktok[dr_, tok:tok + 128, :].rearrange("(c j) f -> j c f", j=32),
                           reads=[self.db("ktok", g)], writes=[c_["kt"]], key=c_["kt"])
                    fw.dma(fw.sp, c_["v"][:, :, :], self.vtok[tok:tok + 128, :].rearrange("(c j) f -> j c f", j=32),
                           reads=[self.db("vtok", g)], writes=[c_["v"]], key=c_["v"])
                    fw.dma(fw.sp, c_["dc"][:, :], self.decT[tok // 128, :, :], reads=[self.db("decT", g)], writes=[c_["dc"]], key=c_["dc"])
                    act.append(ch)
                for ci in range(4):
                    for ch in act:
                        s, dr_ = ch["s"], ch["dr"]
                        c_ = ch["cur"]
                        qT, kT, kt, v, dc, op_ = c_["qT"], c_["kT"], c_["kt"], c_["v"], c_["dc"], c_["op"]
                        c = ci if dr_ == 0 else 3 - ci
                        t0 = c * 32
                        ap_ = aps[cnt["a"] % 2]
                        cnt["a"] += 1
                        sp_ = sps[cnt["s"] % 2]
                        cnt["s"] += 1
                        A = ch["A"][ci % 2]
                        for h in range(4):
                            fw.op(fw.pe, lambda h=h: nc.tensor.matmul(
                                ap_[0:32, h * 32:(h + 1) * 32], kT[:, h, t0:t0 + 32], qT[:, h, t0:t0 + 32], start=True, stop=True),
                                reads=[kT, qT], writes=[ap_])
                        fw.op(fw.dve, lambda: nc.vector.tensor_tensor(
                            A[:, :], ap_[0:32, 0:128], self.maskA[:, dr_ * 128:(dr_ + 1) * 128], ALU.mult),
                            reads=[ap_, self.maskA], writes=[A])
                        for h in range(4):
                            fw.op(fw.pe, lambda h=h: nc.tensor.matmul(
                                sp_[:, h * 128:(h + 1) * 128], kt[:, c, h * 128:(h + 1) * 128], v[:, c, h * 128:(h + 1) * 128],
                                start=True, stop=True), reads=[kt, v], writes=[sp_])
                        for h in range(4):
                            osl = op_[:, h * 128 + t0:h * 128 + t0 + 32]
                            fw.op(fw.pe, lambda h=h, osl=osl: nc.tensor.matmul(
                                osl, ch["Sb"][:, h * 128:(h + 1) * 128], qT[:, h, t0:t0 + 32], start=True, stop=False),
                                reads=[ch["Sb"], qT], writes=[op_])
                            fw.op(fw.pe, lambda h=h, osl=osl: nc.tensor.matmul(
                                osl, v[:, c, h * 128:(h + 1) * 128], A[:, h * 32:(h + 1) * 32], start=False, stop=True),
                                reads=[v, A], writes=[op_])
                        fw.op(fw.dve, lambda: nc.vector.tensor_tensor(ch["tmp"][:, :], ch["S"][:, :], sp_[:, :], ALU.add),
                              reads=[ch["S"], sp_], writes=[ch["tmp"]])
                        dsl = dc[:, dr_ * 16:(dr_ + 1) * 16].rearrange("p (h c) -> p h c", h=4)[:, :, c:c + 1].to_broadcast([128, 4, 128])
                        fw.op(fw.pool, lambda: nc.gpsimd.tensor_tensor(
                            ch["S"][:, :].rearrange("p (h v) -> p h v", h=4), ch["tmp"][:, :].rearrange("p (h v) -> p h v", h=4), dsl, ALU.mult),
                            reads=[ch["tmp"], dc], writes=[ch["S"]])
                        fw.op(fw.act, lambda: nc.scalar.copy(ch["Sb"][:, :], ch["S"][:, :]), reads=[ch["S"]], writes=[ch["Sb"]])
                for ch in act:
                    s, dr_ = ch["s"], ch["dr"]
                    c_ = ch["cur"]
                    bi, nbk, op_ = c_["bi"], c_["nbk"], c_["op"]
                    osb = oacc[s][:, :, bi * 128:(bi + 1) * 128]
                    o3 = op_[:, :].rearrange("p (h t) -> p h t", h=4)
                    fwd_first = bi <= (nbk - 1 - bi)
                    is_first = ((dr_ == 0) == fwd_first) if bi != nbk - 1 - bi else (dr_ == 0)
                    if is_first:
                        fw.op(fw.act, lambda: nc.scalar.copy(osb, o3), reads=[op_], writes=[oacc[s]])
                    else:
                        fw.op(fw.dve, lambda: nc.vector.tensor_tensor(osb, osb, o3, ALU.add),
                              reads=[op_, oacc[s]], writes=[oacc[s]])
            sqb = [fw.sbuf(st, f"fsq{i}", [128, TT], BF16) for i in range(2)]
            rs = [fw.sbuf(st, f"frs{i}", [128, TT], F32) for i in range(2)]
            sgl = [fw.sbuf(st, f"fsg{i}", [128, 4, TT], BF16) for i in range(2)]
            oo = [fw.sbuf(st, f"foo{i}", [128, 4, TT], BF16) for i in range(2)]
            on = [fw.sbuf(st, f"fon{i}", [128, TT], F32) for i in range(2)]
            k = 0
            for g in range(self.T // TT):
                s = self.seq_of(g * TT)
                if s not in grp:
                    continue
                lt = g * TT - self.seq_off[s]
                gi = g % 2
                fw.dma(fw.sp, sgl[gi][:, :, :], self.sgT[:, :, g * TT:(g + 1) * TT], reads=[self.db("sgT", g)], writes=[sgl[gi]], key=sgl[gi])
                for h in range(4):
                    k += 1
                    o_ = oacc[s][:, h, lt:lt + TT]
                    sq, r_, n_ = sqb[k % 2], rs[k % 2], on[k % 2]
                    mp = ops[k % 3]
                    fw.op(fw.act, lambda o_=o_, sq=sq: nc.scalar.activation(sq[:, :], o_, AF.Square), reads=[oacc[s]], writes=[sq])
                    fw.op(fw.pe, lambda sq=sq, mp=mp: nc.tensor.matmul(mp[:, :], self.onesB[:, :], sq[:, :], start=True, stop=True),
                          reads=[sq, self.onesB], writes=[mp])
                    fw.op(fw.act, lambda mp=mp, r_=r_: nc.scalar.activation(r_[:, :], mp[:, :], AF.Ln, bias=epsc[:, 0:1], scale=1.0 / 128),
                          reads=[mp, epsc], writes=[r_])
                    fw.op(fw.act, lambda r_=r_: nc.scalar.activation(r_[:, :], r_[:, :], AF.Exp, scale=-0.5), reads=[r_], writes=[r_])
                    fw.op(fw.dve, lambda o_=o_, r_=r_, n_=n_: nc.vector.tensor_tensor(n_[:, :], o_, r_[:, :], ALU.mult),
                          reads=[oacc[s], r_], writes=[n_])
                    fw.op(fw.dve, lambda n_=n_, h=h, gi=gi: nc.vector.scalar_tensor_tensor(
                        oo[gi][:, h, :], n_[:, :], self.ghgT[:, l:l + 1], sgl[gi][:, h, :], ALU.mult, ALU.mult),
                        reads=[n_, self.ghgT, sgl[gi]], writes=[oo[gi]])
                fw.dma(fw.pool, self.ohgT[:, :, g * TT:(g + 1) * TT], oo[gi][:, :, :], reads=[oo[gi]], writes=[self.db("ohgT", g)], key=oo[gi])
            fw.end_phase()

    def mix_C(self, l):
        nc, fw = self.nc, self.fw
        import os
        LVL = int(os.environ.get("NA_LVL", "9"))
        with contextlib.ExitStack() as st:
            fw.begin_phase()
            G2 = fw.sbuf(st, "G2", [128, 8 * 14 * 64], F32)
            fw.dma(fw.sp, G2[:, :], self.G[l, :, :], writes=[G2], key=G2)
            G4 = G2[:, :].rearrange("p (h d q) -> p h d q", h=8, d=14)
            if LVL >= 0:
              fw.op(fw.dve, lambda: nc.vector.tensor_tensor(
                G2[:, :].rearrange("p (a q) -> p a q", q=64), G2[:, :].rearrange("p (a q) -> p a q", q=64),
                self.NEG[:, :].unsqueeze(1).to_broadcast([128, 8 * 14, 64]), ALU.add), reads=[G2, self.NEG], writes=[G2])
            Lmax = max(self.seqs)
            QTs = fw.sbuf(st, "QTs", [64, 8, Lmax], BF16)
            KTs = fw.sbuf(st, "KTs", [64, 8, Lmax], BF16)
            Vw = [fw.sbuf(st, f"Vw{i}", [128, 4, 512], BF16) for i in range(3)]
            sb = [fw.sbuf(st, f"nsb{i}", [128, 512], F32) for i in range(2)]
            pT = [fw.sbuf(st, f"npT{i}", [128, 512], BF16) for i in range(2)]
            rec = fw.sbuf(st, "nrec", [64, 512], F32)
            ost = [fw.sbuf(st, f"nost{i}", [64, 8, TT], BF16) for i in range(2)]
            stp = [fw.psum(st, f"nst{i}", [128, 512], F32) for i in range(4)]
            nump = [fw.psum(st, f"nnum{i}", [128, 512], F32) for i in range(2)]
            denp = [fw.psum(st, f"nden{i}", [128, 512], F32) for i in range(2)]
            items = []
            nrow = 0
            for s in range(self.NS):
                L = self.seqs[s]
                off = self.seq_off[s]
                rows = L // 64
                for r in range(rows):
                    r0 = min(max(r - 4, 0), rows - 8)
                    for hp in range(4):
                        items.append(dict(s=s, L=L, off=off, rows=rows, r=r, r0=r0, dl=r - r0, hp=hp, nrow=nrow,
                                          k=len(items) + 1, gi=(off + r * 64) // TT))
                    nrow += 1

            def stage1(it):
                s, L, off, r, r0, hp = it["s"], it["L"], it["off"], it["r"], it["r0"], it["hp"]
                if hp == 0:
                    if r == 0:
                        gs = [g for g in range(off // TT, (off + L) // TT)]
                        fw.dma(fw.sp, QTs[:, :, 0:L], self.QT[:, :, off:off + L], reads=[self.db("QT", g) for g in gs], writes=[QTs], key=QTs)
                        fw.dma(fw.sp, KTs[:, :, 0:L], self.KT[:, :, off:off + L], reads=[self.db("KT", g) for g in gs], writes=[KTs], key=KTs)
                    vw = Vw[it["nrow"] % 3]
                    t0 = off + r0 * 64
                    fw.dma(fw.sp, vw[:, :, :], self.Vna[t0:t0 + 512, :].rearrange("(m p) f -> p m f", p=128),
                           reads=[self.db("Vna", g) for g in range(t0 // TT, (t0 + 511) // TT + 1)], writes=[vw], key=vw)
                stq = stp[it["k"] % 4]
                for h2 in range(2):
                    for m in range(4):
                        fw.op(fw.pe, lambda h2=h2, m=m: nc.tensor.matmul(
                            stq[:, (h2 * 4 + m) * 64:(h2 * 4 + m + 1) * 64],
                            KTs[:, 2 * hp + h2, r0 * 64 + m * 128:r0 * 64 + (m + 1) * 128],
                            QTs[:, 2 * hp + h2, r * 64:(r + 1) * 64], start=True, stop=True),
                            reads=[KTs, QTs], writes=[stq])

            def stage2(it):
                off, r, hp, dl, k = it["off"], it["r"], it["hp"], it["dl"], it["k"]
                vw = Vw[it["nrow"] % 3]
                np_, dp_ = nump[it["nrow"] % 2], denp[it["nrow"] % 2]
                gi = it["gi"]
                og = ost[gi % 2]
                stq = stp[k % 4]
                sb_, pT_ = sb[k % 2], pT[k % 2]
                fw.op(fw.dve, lambda: nc.vector.tensor_tensor(
                    sb_[:, :].rearrange("p (h m q) -> p h m q", h=2, m=4),
                    stq[:, :].rearrange("p (h m q) -> p h m q", h=2, m=4),
                    G4[:, 2 * hp:2 * hp + 2, 7 - dl:14 - dl:2, :], ALU.add), reads=[stq, G2], writes=[sb_])
                fw.op(fw.act, lambda: nc.scalar.activation(pT_[:, :], sb_[:, :], AF.Exp), reads=[sb_], writes=[pT_])
                for h2 in range(2):
                    h = 2 * hp + h2
                    for m in range(4):
                        fw.op(fw.pe, lambda h=h, h2=h2, m=m: nc.tensor.matmul(
                            np_[0:64, h * 64:(h + 1) * 64], vw[:, m, h * 64:(h + 1) * 64],
                            pT_[:, (h2 * 4 + m) * 64:(h2 * 4 + m + 1) * 64], start=(m == 0), stop=(m == 3)),
                            reads=[vw, pT_], writes=[np_])
                for m in range(4):
                    fw.op(fw.pe, lambda m=m: nc.tensor.matmul(
                        dp_[0:64, hp * 128:(hp + 1) * 128].rearrange("p (h q) -> p h q", h=2),
                        self.onesB[:, 0:64],
                        pT_[:, :].rearrange("p (h m q) -> p h m q", h=2, m=4)[:, :, m, :], start=(m == 0), stop=(m == 3)),
                        reads=[self.onesB, pT_], writes=[dp_])

            def row_end(it):
                off, r = it["off"], it["r"]
                np_, dp_ = nump[it["nrow"] % 2], denp[it["nrow"] % 2]
                gi = it["gi"]
                og = ost[gi % 2]
                fw.op(fw.act, lambda: nc.scalar.activation(rec[:, :], dp_[0:64, :], AF.Ln), reads=[dp_], writes=[rec])
                fw.op(fw.act, lambda: nc.scalar.activation(rec[:, :], rec[:, :], AF.Exp, scale=-1.0), reads=[rec], writes=[rec])
                tl = (off + r * 64) - gi * TT
                fw.op(fw.dve, lambda: nc.vector.tensor_tensor(
                    og[:, :, tl:tl + 64], np_[0:64, :].rearrange("p (h q) -> p h q", h=8),
                    rec[:, :].rearrange("p (h q) -> p h q", h=8), ALU.mult), reads=[np_, rec], writes=[og])
                if tl + 64 == TT:
                    for par in range(2):
                        fw.dma(fw.pool, self.onaT[par * 64:(par + 1) * 64, :, gi * TT:(gi + 1) * TT], og[:, par::2, :],
                               reads=[og], writes=[self.db("onaT", gi)], key=og)

            stage1(items[0])
            if len(items) > 1:
                stage1(items[1])
            LAG = 2
            for i, it in enumerate(items):
                if i + 2 < len(items):
                    stage1(items[i + 2])
                stage2(it)
                if i - LAG >= 0 and items[i - LAG]["hp"] == 3:
                    row_end(items[i - LAG])
            for j in range(max(0, len(items) - LAG), len(items)):
                if items[j]["hp"] == 3:
                    row_end(items[j])
            fw.end_phase()

    def mix_D(self, l):
        nc, fw = self.nc, self.fw
        with contextlib.ExitStack() as st:
            fw.begin_phase()
            Wo = fw.sbuf(st, "Wo", [128, 8, D], BF16)
            fw.dma(fw.pool, Wo[:, :, :], self.w_out[l, :, :].rearrange("(c p) d -> p c d", p=128), writes=[Wo], key=Wo)
            oh = [fw.sbuf(st, f"doh{i}", [128, 4, TT], BF16) for i in range(2)]
            on_ = [fw.sbuf(st, f"don{i}", [128, 4, TT], BF16) for i in range(2)]
            xr = [fw.sbuf(st, f"dxr{i}", [128, TT], F32) for i in range(4)]
            dps = [fw.psum(st, f"dd{i}", [128, 512], F32) for i in range(4)]
            nxr = 0
            for g in range(self.T // TT):
                s = self.seq_of(g * TT)
                gi = g % 2
                tsl = slice(g * TT, (g + 1) * TT)
                fw.dma(fw.sp, oh[gi][:, :, :], self.ohgT[:, :, tsl], reads=[self.db("ohgT", g)], writes=[oh[gi]], key=oh[gi])
                fw.dma(fw.sp, on_[gi][:, :, :], self.onaT[:, :, tsl], reads=[self.db("onaT", g)], writes=[on_[gi]], key=on_[gi])
                for c in range(8):
                    dp = dps[nxr % 4]
                    x_ = xr[nxr % 4]
                    nxr += 1
                    xb = self.db("xT", g, c)
                    fw.dma(fw.sp, x_[:, :], self.xT[:, c, tsl], reads=[xb], writes=[x_], key=x_)
                    for kc in range(8):
                        src = oh[gi] if kc < 4 else on_[gi]
                        fw.op(fw.pe, lambda kc=kc, src=src: nc.tensor.matmul(
                            dp[:, :], Wo[:, kc, c * 128:(c + 1) * 128], src[:, kc % 4, :], start=(kc == 0), stop=(kc == 7)),
                            reads=[Wo, src], writes=[dp])
                    fw.op(fw.dve, lambda: nc.vector.scalar_tensor_tensor(
                        x_[:, :], dp[:, :], self.g5[:, l, 1, s, c:c + 1], x_[:, :], ALU.mult, ALU.add),
                        reads=[dp, x_, self.gcol], writes=[x_])
                    fw.dma(fw.act, self.xT[:, c, tsl], x_[:, :], reads=[x_], writes=[xb], key=x_)
            fw.end_phase()


def _core_inputs(inp, xcore, ccore, depth, consts, G):
    d = {
        "x": np.ascontiguousarray(xcore, dtype=np.float32),
        "c": np.ascontiguousarray(ccore, dtype=np.float32).reshape(-1, 128),
        "w_mod": inp["w_mod"][:depth],
        "b_mod": np.ascontiguousarray(inp["b_mod"][:depth]).reshape(-1, 128),
        "norm_g": np.ascontiguousarray(inp["norm_g"][:depth]).reshape(-1, 128),
        "ffn_w_gate": inp["ffn_w_gate"][:depth],
        "ffn_w_up": inp["ffn_w_up"][:depth],
        "ffn_w_down": inp["ffn_w_down"][:depth],
        "w_in": inp["w_in"][:depth],
        "w_out": inp["w_out"][:depth],
        "hg_lb_fwd": np.ascontiguousarray(inp["hg_lb_fwd"][:depth]).reshape(1, -1),
        "hg_lb_bwd": np.ascontiguousarray(inp["hg_lb_bwd"][:depth]).reshape(1, -1),
        "hg_norm_g": inp["hg_norm_g"][:depth],
        "na_q_norm_g": np.ascontiguousarray(inp["na_q_norm_g"][:depth]).reshape(1, -1),
        "na_k_norm_g": np.ascontiguousarray(inp["na_k_norm_g"][:depth]).reshape(1, -1),
        "G": G[:depth].reshape(depth, 128, -1),
    }
    for k, v in consts.items():
        d["c_" + k] = v
    return {k: np.ascontiguousarray(np.asarray(v, dtype=np.float32)) for k, v in d.items()}


def kernel(**inputs):
    inp = {k: np.asarray(v) for k, v in inputs.items()}
    xp, xs = inp["x_prompt"], inp["x_sample"]
    cp, cs_ = inp["c_prompt"], inp["c_sample"]
    seqs = [xp.shape[1], xp.shape[1], xs.shape[1]]
    prog = Prog(seqs, DEPTH)
    nc = prog.build()
    consts = _consts()
    G = _expand_rpb(inp["na_rpb"].astype(np.float32))
    in_maps = []
    for i in range(N_CORES):
        xcore = np.concatenate([xp[2 * i], xp[2 * i + 1], xs[i]], axis=0)
        ccore = np.stack([cp[2 * i], cp[2 * i + 1], cs_[i]], axis=0)
        in_maps.append(_core_inputs(inp, xcore, ccore, DEPTH, consts, G))
    res = run_bass_kernel_spmd(nc, in_maps, core_ids=list(range(N_CORES)))
    Lp, Ls = xp.shape[1], xs.shape[1]
    yp = np.empty(xp.shape, np.float32)
    ys = np.empty(xs.shape, np.float32)
    for i in range(N_CORES):
        y = np.asarray(res.results[i]["y"], dtype=np.float32)
        yp[2 * i] = y[0:Lp]
        yp[2 * i + 1] = y[Lp:2 * Lp]
        ys[i] = y[2 * Lp:2 * Lp + Ls]
    return (yp, ys)
```

```python
import contextlib
import numpy as np
import ml_dtypes
import concourse.bass as bass
import concourse.mybir as mybir
from concourse.bass_utils import run_bass_kernel_spmd

F32 = mybir.dt.float32
BF16 = mybir.dt.bfloat16
AF = mybir.ActivationFunctionType
ALU = mybir.AluOpType
AX = mybir.AxisListType

D = 1024
DFF = 2816
NF = DFF // 128
DEPTH = 4
EPS = 1e-6
INW = 4096
N_CORES = 8
TT = 512

SEM_M = 1 << 30
DSEM_LIM = 1 << 30


class Buf:
    __slots__ = ("name", "t", "last_w", "readers", "excl", "dsem")

    def __init__(self, name, t=None, excl=False):
        self.name = name
        self.t = t
        self.last_w = None
        self.readers = {}
        self.excl = excl
        self.dsem = {}

    def __getitem__(self, k):
        return self.t[k]


class DSem:
    __slots__ = ("sem", "issued", "closed")

    def __init__(self, sem):
        self.sem = sem
        self.issued = 0
        self.closed = False


class Eng:
    def __init__(self, fw, name, eng, is_pe=False):
        self.fw = fw
        self.name = name
        self.eng = eng
        self.is_pe = is_pe
        self.n = 0
        self.sems = []
        self.known = {}

    def sem_for(self, n):
        k = (n - 1) // SEM_M
        while len(self.sems) <= k:
            self.sems.append(self.fw.nc.alloc_semaphore(name=f"s_{self.name}_{len(self.sems)}"))
        return self.sems[k], n - k * SEM_M


class FW:
    def __init__(self, nc):
        self.nc = nc
        self.pe = Eng(self, "pe", nc.tensor, is_pe=True)
        self.act = Eng(self, "act", nc.scalar)
        self.dve = Eng(self, "dve", nc.vector)
        self.pool = Eng(self, "pool", nc.gpsimd)
        self.sp = Eng(self, "sp", nc.sync)
        self.engs = [self.pe, self.act, self.dve, self.pool, self.sp]
        self.dsems = []
        self.free_dsems = {"hw": [], "sw": []}
        self.cur = []
        self.nwaits = 0
        self.nins = 0
        self.ndma = 0
        self._uid = 0

    def sbuf(self, st, name, shape, dt):
        self._uid += 1
        b = Buf(name, st.enter_context(self.nc.sbuf_tensor(f"{name}_{self._uid}", list(shape), dt)))
        self.cur.append(b)
        return b

    def begin_phase(self):
        self.cur = []

    def end_phase(self):
        self.barrier()
        for b in self.cur:
            for kind, ds in b.dsem.items():
                ds.closed = True
                self.free_dsems[kind].append(ds)
            b.dsem = {}
        self.cur = []

    def psum(self, st, name, shape, dt=F32):
        self._uid += 1
        return Buf(name, st.enter_context(self.nc.psum_tensor(f"{name}_{self._uid}", list(shape), dt)), excl=True)

    def dram(self, name, shape, dt, kind="Internal"):
        return self.nc.dram_tensor(name, list(shape), dt, kind=kind).ap()

    def _wait(self, e, tk):
        kind, obj, val = tk
        if kind == "e":
            if obj is e and e.is_pe:
                return
            if e.known.get(obj, 0) >= val:
                return
            e.known[obj] = val
            sem, v = obj.sem_for(val)
            e.eng.wait_ge(sem, v)
            self.nwaits += 1
        else:
            ds = obj
            val = ds.issued
            ds.closed = True
            if e.known.get(ds, 0) >= val:
                return
            e.known[ds] = val
            e.eng.wait_ge(ds.sem, val)
            self.nwaits += 1

    def _deps(self, e, reads, writes):
        need = []
        for b in reads:
            if b.last_w is not None:
                need.append(b.last_w)
            if b.excl:
                for k, t in b.readers.items():
                    if k is not e:
                        need.append(t)
        for b in writes:
            if b.last_w is not None:
                need.append(b.last_w)
            need.extend(b.readers.values())
        for t in need:
            self._wait(e, t)

    def _record(self, tk, key, reads, writes):
        for b in reads:
            b.readers[key] = tk
        for b in writes:
            b.last_w = tk
            b.readers = {}

    def op(self, e, fn, reads=(), writes=()):
        self._deps(e, reads, writes)
        ins = fn()
        e.n += 1
        sem, _ = e.sem_for(e.n)
        ins.then_inc(sem, 1)
        self._record(("e", e, e.n), e, reads, writes)
        self.nins += 1
        return ins

    def dma(self, q, out, in_, reads=(), writes=(), key=None, **kw):
        self._deps(q, reads, writes)
        kind = "sw" if q is self.pool else "hw"
        ds = key.dsem.get(kind)
        if ds is not None and ds.issued + 16 > DSEM_LIM:
            self._wait(q, ("d", ds, ds.issued))
            ds = None
        if ds is None:
            fl = self.free_dsems[kind]
            while fl and ds is None:
                cand = fl.pop()
                if cand.issued + 16 <= DSEM_LIM:
                    ds = cand
            if ds is None:
                self._uid += 1
                ds = DSem(self.nc.alloc_semaphore(name=f"d{kind}_{self._uid}"))
                self.dsems.append(ds)
            key.dsem[kind] = ds
        if ds.closed:
            self._wait(q, ("d", ds, ds.issued))
            ds.closed = False
        ins = q.eng.dma_start(out=out, in_=in_, **kw)
        ins.then_inc(ds.sem, 16)
        ds.issued += 16
        self._record(("d", ds, ds.issued), ds, reads, writes)
        self.ndma += 1
        return ins

    def barrier(self):
        sp = self.sp
        for ds in self.dsems:
            if ds.issued:
                self._wait(sp, ("d", ds, ds.issued))
        for e in self.engs:
            if e is not sp and e.n:
                self._wait(sp, ("e", e, e.n))
        ins = sp.eng.nop()
        sp.n += 1
        sem, _ = sp.sem_for(sp.n)
        ins.then_inc(sem, 1)
        for e in self.engs:
            if e is not sp:
                self._wait(e, ("e", sp, sp.n))

    def finish(self):
        self.barrier()


def _consts():
    c = {}
    c["identF"] = np.eye(128, dtype=np.float32)
    j = np.arange(128)[:, None]
    i = np.arange(128)[None, :]
    same = (j // 32) == (i // 32)
    c["Tf"] = (same & (j <= i)).astype(np.float32)
    c["Tb"] = (same & (j >= i)).astype(np.float32)
    ind = np.zeros((128, 4), np.float32)
    ind[np.arange(128), np.arange(128) // 32] = 1.0
    c["Ind"] = ind
    jj = np.arange(32)[:, None]
    ii = np.arange(32)[None, :]
    m = np.zeros((32, 2, 4, 32), np.float32)
    m[:, 0] = (jj <= ii).astype(np.float32)[:, None, :]
    m[:, 1] = (jj >= ii).astype(np.float32)[:, None, :]
    c["maskA"] = m.reshape(32, 256)
    c["ones"] = np.ones((128, 128), np.float32)
    q = np.arange(64)
    c0 = np.clip(q - 8, 0, 48)
    cm = (q[None, :] >= c0[:, None]) & (q[None, :] < c0[:, None] + 16)
    neg = np.where(cm.T, 0.0, -30000.0).astype(np.float32)
    c["NEG"] = np.concatenate([neg, neg], axis=0)
    return c


def _expand_rpb(rpb):
    L = rpb.shape[0]
    kc = np.arange(64)[:, None]
    qc = np.arange(64)[None, :]
    rel = np.clip(kc - qc, -15, 15) + 15
    G = np.empty((L, 128, 8, 14, 64), np.float32)
    for e in range(2):
        sub = rpb[:, :, e:e + 14, :]
        g = sub[:, :, :, rel]
        G[:, e * 64:(e + 1) * 64] = np.transpose(g, (0, 3, 1, 2, 4))
    return G


class Prog:
    def __init__(self, seqs, depth, stop_after=None):
        self.seqs = list(seqs)
        self.NS = len(seqs)
        self.T = sum(seqs)
        self.depth = depth
        self.stop_after = stop_after
        self.seq_off = [sum(seqs[:i]) for i in range(self.NS)]
        self.nc = bass.Bass("TRN2", target_bir_lowering=False)
        self.fw = FW(self.nc)
        self.dbufs = {}

    def db(self, *key):
        b = self.dbufs.get(key)
        if b is None:
            b = Buf("db_" + "_".join(str(k) for k in key))
            self.dbufs[key] = b
        return b

    def seq_of(self, tok):
        for s in range(self.NS):
            if self.seq_off[s] <= tok < self.seq_off[s] + self.seqs[s]:
                return s
        raise ValueError

    def build(self):
        nc, fw = self.nc, self.fw
        T, NS, depth = self.T, self.NS, self.depth
        dr = fw.dram
        I = "ExternalInput"
        self.x = dr("x", [T, D], F32, I)
        self.c = dr("c", [NS * 8, 128], F32, I)
        self.w_mod = dr("w_mod", [depth, D, 9 * D], F32, I)
        self.b_mod = dr("b_mod", [depth * 72, 128], F32, I)
        self.norm_g = dr("norm_g", [depth * 24, 128], F32, I)
        self.wg = dr("ffn_w_gate", [depth, 2, D, DFF], F32, I)
        self.wu = dr("ffn_w_up", [depth, 2, D, DFF], F32, I)
        self.wd = dr("ffn_w_down", [depth, 2, DFF, D], F32, I)
        self.w_in = dr("w_in", [depth, D, INW], F32, I)
        self.w_out = dr("w_out", [depth, D, D], F32, I)
        self.lbf = dr("hg_lb_fwd", [1, depth * 512], F32, I)
        self.lbb = dr("hg_lb_bwd", [1, depth * 512], F32, I)
        self.hgg = dr("hg_norm_g", [depth, 128], F32, I)
        self.gq = dr("na_q_norm_g", [1, depth * 64], F32, I)
        self.gk = dr("na_k_norm_g", [1, depth * 64], F32, I)
        self.G = dr("G", [depth, 128, 8 * 14 * 64], F32, I)
        cs = _consts()
        self.cd = {k: dr("c_" + k, list(v.shape), F32, I) for k, v in cs.items()}
        self.y = dr("y", [T, D], F32, "ExternalOutput")
        self.xT = dr("xT", [128, 8, T], F32)
        self.qtT = dr("qtT", [2, 128, 4, T], BF16)
        self.ktT = dr("ktT", [2, 128, 4, T], BF16)
        self.ktok = dr("ktok", [2, T, 512], BF16)
        self.vtok = dr("vtok", [T, 512], BF16)
        self.decT = dr("decT", [T // 128, 128, 32], F32)
        self.sgT = dr("sgT", [128, 4, T], BF16)
        self.QT = dr("QT", [64, 8, T], BF16)
        self.KT = dr("KT", [64, 8, T], BF16)
        self.Vna = dr("Vna", [T, 512], BF16)
        self.ohgT = dr("ohgT", [128, 4, T], BF16)
        self.onaT = dr("onaT", [128, 4, T], BF16)
        self.lbD = dr("lbD", [1, depth * 1024], F32)

        with contextlib.ExitStack() as pst:
            self.pst = pst
            self.setup()
            self.phase_in()
            done = False
            for l in range(depth):
                with contextlib.ExitStack() as wst:
                    Wpre = None
                    for sub in range(3):
                        if sub == 1:
                            self.mixer(l, "ABC")
                            if self.stop_after is None or tuple(self.stop_after) != (l, 1):
                                Wpre = self.ffn_weights(wst, l, 1)
                            self.mixer(l, "D")
                        else:
                            self.ffn(l, sub, W=(Wpre if sub == 2 else None))
                        if self.stop_after is not None and (l, sub) == tuple(self.stop_after):
                            done = True
                            break
                if done:
                    break
            self.phase_out()
            fw.finish()
        return nc

    def load_const(self, st, name, shape, dt, src, q=None):
        fw = self.fw
        b = fw.sbuf(st, name, shape, dt)
        q = q or (fw.pool if dt == BF16 else fw.sp)
        fw.dma(q, b[tuple(slice(None) for _ in shape)], src, writes=[b], key=b)
        return b

    def setup(self):
        nc, fw, pst = self.nc, self.fw, self.pst
        NS, depth = self.NS, self.depth
        cd = self.cd
        self.identF = self.load_const(pst, "identF", [128, 128], F32, cd["identF"][:, :])
        self.identB = self.load_const(pst, "identB", [128, 128], BF16, cd["identF"][:, :])
        self.TfB = self.load_const(pst, "TfB", [128, 128], BF16, cd["Tf"][:, :])
        self.TbB = self.load_const(pst, "TbB", [128, 128], BF16, cd["Tb"][:, :])
        self.IndB = self.load_const(pst, "IndB", [128, 4], BF16, cd["Ind"][:, :])
        self.maskA = self.load_const(pst, "maskA", [32, 256], F32, cd["maskA"][:, :])
        self.onesB = self.load_const(pst, "onesB", [128, 128], BF16, cd["ones"][:, :])
        self.onesF = self.load_const(pst, "onesF", [128, 128], F32, cd["ones"][:, :])
        self.NEG = self.load_const(pst, "NEG", [128, 64], F32, cd["NEG"][:, :])
        self.modT = fw.sbuf(pst, "modT", [128, depth * 72 * NS], F32)
        self.gT = fw.sbuf(pst, "gT", [128, depth * 24], F32)
        self.Acol = fw.sbuf(pst, "Acol", [128, depth * 3 * NS * 8], F32)
        self.gcol = fw.sbuf(pst, "gcol", [128, depth * 3 * NS * 8], F32)
        self.ghgT = fw.sbuf(pst, "ghgT", [128, depth], F32)
        self.gqrow = fw.sbuf(pst, "gqrow", [1, depth * 64], F32)
        self.gkrow = fw.sbuf(pst, "gkrow", [1, depth * 64], F32)

        with contextlib.ExitStack() as st:
            fw.begin_phase()
            ps = [fw.psum(st, f"sps{i}", [128, 512], F32) for i in range(3)]
            bmodT = fw.sbuf(st, "bmodT", [128, depth * 72], F32)
            scT = fw.sbuf(st, "scT", [128, NS * 8], F32)
            rows = fw.sbuf(st, "rows", [96, 128], F32)
            crow = fw.sbuf(st, "crow", [NS * 8, 128], F32)
            crow2 = fw.sbuf(st, "crow2", [NS * 8, 128], F32)
            hrow = fw.sbuf(st, "hrow", [depth, 128], F32)
            self.lbrow = fw.sbuf(st, "lbrow", [1, depth * 1024], F32)

            def tr_rows(src_ap, n, dst, dst_off, tmp):
                fw.dma(fw.sp, tmp[0:n, :], src_ap, writes=[tmp], key=tmp)
                fw.op(fw.pe, lambda: nc.tensor.transpose(ps[0][:, 0:n], tmp[0:n, :], self.identF[0:n, 0:n]),
                      reads=[tmp, self.identF], writes=[ps[0]])
                fw.op(fw.dve, lambda: nc.vector.tensor_copy(dst[:, dst_off:dst_off + n], ps[0][:, 0:n]),
                      reads=[ps[0]], writes=[dst])

            for r0 in range(0, depth * 72, 96):
                n = min(96, depth * 72 - r0)
                tr_rows(self.b_mod[r0:r0 + n, :], n, bmodT, r0, rows)
            for r0 in range(0, depth * 24, 96):
                n = min(96, depth * 24 - r0)
                tr_rows(self.norm_g[r0:r0 + n, :], n, self.gT, r0, rows)
            fw.dma(fw.sp, hrow[:, :], self.hgg[:, :], writes=[hrow], key=hrow)
            fw.op(fw.pe, lambda: nc.tensor.transpose(ps[1][:, 0:depth], hrow[:, :], self.identF[0:depth, 0:depth]),
                  reads=[hrow, self.identF], writes=[ps[1]])
            fw.op(fw.dve, lambda: nc.vector.tensor_scalar(self.ghgT[:, :], ps[1][:, 0:depth], 0.5, None, ALU.mult),
                  reads=[ps[1]], writes=[self.ghgT])
            fw.dma(fw.sp, crow[:, :], self.c[:, :], writes=[crow], key=crow)
            fw.op(fw.act, lambda: nc.scalar.activation(crow2[:, :], crow[:, :], AF.Silu), reads=[crow], writes=[crow2])
            fw.op(fw.pe, lambda: nc.tensor.transpose(ps[2][:, 0:NS * 8], crow2[:, :], self.identF[0:NS * 8, 0:NS * 8]),
                  reads=[crow2, self.identF], writes=[ps[2]])
            fw.op(fw.dve, lambda: nc.vector.tensor_copy(scT[:, :], ps[2][:, 0:NS * 8]), reads=[ps[2]], writes=[scT])
            scT3 = scT[:, :].rearrange("p (s k) -> p s k", s=NS)
            fw.dma(fw.sp, self.gqrow[:, :], self.gq[:, :], writes=[self.gqrow], key=self.gqrow)
            fw.dma(fw.sp, self.gkrow[:, :], self.gk[:, :], writes=[self.gkrow], key=self.gkrow)
            praw = fw.sbuf(st, "praw", [1, depth * 1024], F32)
            pr4 = praw[:, :].rearrange("o (l d k) -> o l d k", l=depth, d=2)
            fw.dma(fw.sp, pr4[:, :, 0, :], self.lbf[:, :].rearrange("o (l k) -> o l k", l=depth), writes=[praw], key=praw)
            fw.dma(fw.sp, pr4[:, :, 1, :], self.lbb[:, :].rearrange("o (l k) -> o l k", l=depth), writes=[praw], key=praw)
            mx = fw.sbuf(st, "mx", [1, 1024], F32)
            sm = fw.sbuf(st, "sm", [1, 1024], F32)
            pe_ = fw.sbuf(st, "pexp", [1, depth * 1024], F32)
            pl = lambda t, l: t[:, l * 1024:(l + 1) * 1024]
            fw.op(fw.dve, lambda: nc.vector.tensor_copy(mx[:, :], pl(praw, 0)), reads=[praw], writes=[mx])
            for l in range(1, depth):
                fw.op(fw.dve, lambda l=l: nc.vector.tensor_tensor(mx[:, :], mx[:, :], pl(praw, l), ALU.max),
                      reads=[praw, mx], writes=[mx])
            for l in range(depth):
                fw.op(fw.dve, lambda l=l: nc.vector.tensor_tensor(pl(pe_, l), pl(praw, l), mx[:, :], ALU.subtract),
                      reads=[praw, mx], writes=[pe_])
            fw.op(fw.act, lambda: nc.scalar.activation(pe_[:, :], pe_[:, :], AF.Exp), reads=[pe_], writes=[pe_])
            fw.op(fw.dve, lambda: nc.vector.tensor_copy(sm[:, :], pl(pe_, 0)), reads=[pe_], writes=[sm])
            for l in range(1, depth):
                fw.op(fw.dve, lambda l=l: nc.vector.tensor_tensor(sm[:, :], sm[:, :], pl(pe_, l), ALU.add),
                      reads=[pe_, sm], writes=[sm])
            fw.op(fw.dve, lambda: nc.vector.reciprocal(sm[:, :], sm[:, :]), reads=[sm], writes=[sm])
            fw.op(fw.dve, lambda: nc.vector.memset(pl(self.lbrow, 0), 0.0), writes=[self.lbrow])
            for l in range(1, depth):
                fw.op(fw.dve, lambda l=l: nc.vector.tensor_tensor(pl(pe_, l), pl(pe_, l), sm[:, :], ALU.mult),
                      reads=[pe_, sm], writes=[pe_])
                if l == 1:
                    fw.op(fw.dve, lambda l=l: nc.vector.tensor_copy(pl(self.lbrow, 1), pl(pe_, 1)),
                          reads=[pe_], writes=[self.lbrow])
                else:
                    fw.op(fw.dve, lambda l=l: nc.vector.tensor_tensor(pl(self.lbrow, l), pl(self.lbrow, l - 1), pl(pe_, l), ALU.add),
                          reads=[pe_, self.lbrow], writes=[self.lbrow])

            fw.dma(fw.sp, self.lbD[:, :], self.lbrow[:, :], reads=[self.lbrow], writes=[self.db("lbD")], key=self.lbrow)
            wm = [fw.sbuf(st, f"wm{i}", [128, 8, 1152], F32) for i in range(2)]
            mps = fw.psum(st, "mps", [128, 512], F32)
            modT4 = self.modT[:, :].rearrange("p (l j s) -> p l j s", l=depth, j=72)
            k = 0
            for l in range(depth):
                for pc in range(8):
                    w = wm[k % 2]
                    k += 1
                    fw.dma(fw.sp, w[:, :, :], self.w_mod[l, :, pc * 1152:(pc + 1) * 1152].rearrange("(c p) j -> p c j", p=128),
                           writes=[w], key=w)
                    for jj in range(9):
                        jc = pc * 9 + jj
                        for ck in range(8):
                            fw.op(fw.pe, lambda w=w, jj=jj, jc=jc, ck=ck: nc.tensor.matmul(
                                mps[:, jc * NS:(jc + 1) * NS], w[:, ck, jj * 128:(jj + 1) * 128], scT3[:, :, ck],
                                start=(ck == 0), stop=(ck == 7)), reads=[w, scT], writes=[mps])
                fw.op(fw.dve, lambda l=l: nc.vector.tensor_tensor(
                    modT4[:, l], mps[:, 0:72 * NS].rearrange("p (j s) -> p j s", s=NS),
                    bmodT[:, l * 72:(l + 1) * 72].unsqueeze(2).to_broadcast([128, 72, NS]), ALU.add),
                    reads=[mps, bmodT], writes=[self.modT])
            tmp8 = fw.sbuf(st, "tmp8", [128, 8], F32)
            A5 = self.Acol[:, :].rearrange("p (l u s c) -> p l u s c", l=depth, u=3, s=NS)
            g5 = self.gcol[:, :].rearrange("p (l u s c) -> p l u s c", l=depth, u=3, s=NS)
            for l in range(depth):
                for u in range(3):
                    for s in range(NS):
                        sc = modT4[:, l, u * 24 + 8:u * 24 + 16, s]
                        gt = modT4[:, l, u * 24 + 16:u * 24 + 24, s]
                        fw.op(fw.dve, lambda sc=sc: nc.vector.tensor_scalar(tmp8[:, :], sc, 1.0, None, ALU.add),
                              reads=[self.modT], writes=[tmp8])
                        fw.op(fw.dve, lambda l=l, u=u, s=s: nc.vector.tensor_tensor(
                            A5[:, l, u, s, :], tmp8[:, :], self.gT[:, (l * 3 + u) * 8:(l * 3 + u) * 8 + 8], ALU.mult),
                            reads=[tmp8, self.gT], writes=[self.Acol])
                        fac = 1.0 if u == 1 else 0.5
                        fw.op(fw.dve, lambda l=l, u=u, s=s, gt=gt, fac=fac: nc.vector.tensor_scalar(
                            g5[:, l, u, s, :], gt, fac, None, ALU.mult), reads=[self.modT], writes=[self.gcol])
            fw.end_phase()
        self.modT4 = self.modT[:, :].rearrange("p (l j s) -> p l j s", l=depth, j=72)
        self.A5 = self.Acol[:, :].rearrange("p (l u s c) -> p l u s c", l=depth, u=3, s=NS)
        self.g5 = self.gcol[:, :].rearrange("p (l u s c) -> p l u s c", l=depth, u=3, s=NS)

    def phase_in(self):
        nc, fw = self.nc, self.fw
        with contextlib.ExitStack() as st:
            fw.begin_phase()
            xin = [fw.sbuf(st, f"xin{i}", [128, D], F32) for i in range(3)]
            stg = [fw.sbuf(st, f"stg{i}", [128, 8, TT], F32) for i in range(2)]
            ps = [fw.psum(st, f"ips{i}", [128, 512], F32) for i in range(4)]
            nb = 0
            for g in range(self.T // TT):
                sg = stg[g % 2]
                for b in range(4):
                    xi = xin[nb % 3]
                    tok = g * TT + b * 128
                    fw.dma(fw.sp, xi[:, :], self.x[tok:tok + 128, :], writes=[xi], key=xi)
                    for hf in range(2):
                        p = ps[(nb * 2 + hf) % 4]
                        for cc in range(4):
                            c = hf * 4 + cc
                            fw.op(fw.pe, lambda p=p, cc=cc, c=c, xi=xi: nc.tensor.transpose(
                                p[:, cc * 128:(cc + 1) * 128], xi[:, c * 128:(c + 1) * 128], self.identF[:, :]),
                                reads=[xi, self.identF], writes=[p])
                        e = fw.act if hf == 0 else fw.dve
                        dst = sg[:, hf * 4:(hf + 1) * 4, b * 128:(b + 1) * 128]
                        src = p[:, :].rearrange("p (c t) -> p c t", c=4)
                        if hf == 0:
                            fw.op(e, lambda dst=dst, src=src: nc.scalar.copy(dst, src), reads=[p], writes=[sg])
                        else:
                            fw.op(e, lambda dst=dst, src=src: nc.vector.tensor_copy(dst, src), reads=[p], writes=[sg])
                    nb += 1
                fw.dma(fw.pool, self.xT[:, :, g * TT:(g + 1) * TT], sg[:, :, :], reads=[sg],
                       writes=[self.db("xT", g, c) for c in range(8)], key=sg)
            fw.end_phase()

    def phase_out(self):
        nc, fw = self.nc, self.fw
        with contextlib.ExitStack() as st:
            fw.begin_phase()
            xa = [fw.sbuf(st, f"oxa{i}", [128, 8, TT], F32) for i in range(2)]
            yo = [fw.sbuf(st, f"yo{i}", [128, D], F32) for i in range(3)]
            ps = [fw.psum(st, f"ops{i}", [128, 512], F32) for i in range(4)]
            nb = 0
            for g in range(self.T // TT):
                xg = xa[g % 2]
                fw.dma(fw.sp, xg[:, :, :], self.xT[:, :, g * TT:(g + 1) * TT],
                       reads=[self.db("xT", g, c) for c in range(8)], writes=[xg], key=xg)
                for b in range(4):
                    yb = yo[nb % 3]
                    tok = g * TT + b * 128
                    for hf in range(2):
                        p = ps[(nb * 2 + hf) % 4]
                        for cc in range(4):
                            c = hf * 4 + cc
                            fw.op(fw.pe, lambda p=p, cc=cc, c=c, xg=xg, b=b: nc.tensor.transpose(
                                p[:, cc * 128:(cc + 1) * 128], xg[:, c, b * 128:(b + 1) * 128], self.identF[:, :]),
                                reads=[xg, self.identF], writes=[p])
                        if hf == 0:
                            fw.op(fw.act, lambda p=p, yb=yb: nc.scalar.copy(yb[:, 0:512], p[:, :]), reads=[p], writes=[yb])
                        else:
                            fw.op(fw.dve, lambda p=p, yb=yb: nc.vector.tensor_copy(yb[:, 512:1024], p[:, :]), reads=[p], writes=[yb])
                    fw.dma(fw.pool, self.y[tok:tok + 128, :], yb[:, :], reads=[yb], writes=[self.db("y", tok)], key=yb)
                    nb += 1
            fw.end_phase()

    def prep(self, g, l, sub, B, mode, stats_only=False):
        nc, fw = self.nc, self.fw
        s = self.seq_of(g * TT)
        xa, hT, msp = B["xa"], B["hT"], B["msp"]
        fw.dma(fw.sp, xa[:, :, :], self.xT[:, :, g * TT:(g + 1) * TT],
               reads=[self.db("xT", g, c) for c in range(8)], writes=[xa], key=xa)
        for c in range(8):
            sq = B["sq"][c % len(B["sq"])]
            fw.op(fw.act, lambda c=c, sq=sq: nc.scalar.activation(sq[:, :], xa[:, c, :], AF.Square), reads=[xa], writes=[sq])
            fw.op(fw.pe, lambda c=c, sq=sq: nc.tensor.matmul(msp[:, :], self.onesB[:, :], sq[:, :], start=(c == 0), stop=(c == 7)),
                  reads=[sq, self.onesB], writes=[msp])
        rstd = B["rstd"]
        if mode == "sqrt":
            fw.op(fw.act, lambda: nc.scalar.activation(rstd[:, :], msp[:, :], AF.Sqrt, bias=B["epsc"][:, 0:1], scale=1.0 / D),
                  reads=[msp, B["epsc"]], writes=[rstd])
            fw.op(fw.dve, lambda: nc.vector.reciprocal(rstd[:, :], rstd[:, :]), reads=[rstd], writes=[rstd])
        else:
            fw.op(fw.act, lambda: nc.scalar.activation(rstd[:, :], msp[:, :], AF.Ln, bias=B["epsc"][:, 0:1], scale=1.0 / D),
                  reads=[msp, B["epsc"]], writes=[rstd])
            fw.op(fw.act, lambda: nc.scalar.activation(rstd[:, :], rstd[:, :], AF.Exp, scale=-0.5), reads=[rstd], writes=[rstd])
        if not stats_only:
            self.prep_mod(g, l, sub, B, mode, range(8))

    def prep_mod(self, g, l, sub, B, mode, chunks):
        nc, fw = self.nc, self.fw
        s = self.seq_of(g * TT)
        xa, hT, rstd = B["xa"], B["hT"], B["rstd"]
        for c in chunks:
            t = B["tmp"][c % len(B["tmp"])]
            if mode == "sqrt":
                fw.op(fw.dve, lambda c=c, t=t: nc.vector.tensor_tensor(t[:, :], xa[:, c, :], rstd[:, :], ALU.mult),
                      reads=[xa, rstd], writes=[t])
                fw.op(fw.act, lambda c=c, t=t: nc.scalar.activation(
                    hT[:, c, :], t[:, :], AF.Identity, bias=self.modT4[:, l, sub * 24 + c, s:s + 1],
                    scale=self.A5[:, l, sub, s, c:c + 1]), reads=[t, self.modT, self.Acol], writes=[hT])
            else:
                fw.op(fw.dve, lambda c=c, t=t: nc.vector.scalar_tensor_tensor(
                    t[:, :], xa[:, c, :], self.A5[:, l, sub, s, c:c + 1], rstd[:, :], ALU.mult, ALU.mult),
                    reads=[xa, rstd, self.Acol], writes=[t])
                fw.op(fw.dve, lambda c=c, t=t: nc.vector.tensor_scalar(
                    hT[:, c, :], t[:, :], self.modT4[:, l, sub * 24 + c, s:s + 1], None, ALU.add),
                    reads=[t, self.modT], writes=[hT])

    def epsc(self, st):
        fw, nc = self.fw, self.nc
        e = fw.sbuf(st, "epsc", [128, 1], F32)
        fw.op(fw.dve, lambda: nc.vector.memset(e[:, :], EPS), writes=[e])
        return e

    GU_PIECES = ((0, 4), (4, 7), (11, 11))

    @staticmethod
    def gu_piece(f):
        return (0, f) if f < 4 else ((1, f - 4) if f < 11 else (2, f - 11))

    def ffn_weights(self, st, l, which):
        fw = self.fw
        Wg = [fw.sbuf(st, f"Wg{i}", [128, 8, n * 128], BF16) for i, (f0, n) in enumerate(self.GU_PIECES)]
        Wu = [fw.sbuf(st, f"Wu{i}", [128, 8, n * 128], BF16) for i, (f0, n) in enumerate(self.GU_PIECES)]
        Wd = [fw.sbuf(st, f"Wd{i}", [128, NF // 2, D], BF16) for i in range(2)]
        for i, (f0, n) in enumerate(self.GU_PIECES):
            for wt, src in ((Wg, self.wg), (Wu, self.wu)):
                fw.dma(fw.pool, wt[i][:, :, :],
                       src[l, which, :, f0 * 128:(f0 + n) * 128].rearrange("(c p) f -> p c f", p=128),
                       writes=[wt[i]], key=wt[i])
        for hf in range(2):
            fw.dma(fw.pool, Wd[hf][:, :, :],
                   self.wd[l, which, hf * 11 * 128:(hf + 1) * 11 * 128, :].rearrange("(c p) d -> p c d", p=128),
                   writes=[Wd[hf]], key=Wd[hf])
        return Wg, Wu, Wd

    def ffn(self, l, sub, W=None):
        nc, fw = self.nc, self.fw
        which = 0 if sub == 0 else 1
        NG = self.T // TT
        with contextlib.ExitStack() as st:
            fw.begin_phase()
            Wg, Wu, Wd = W if W is not None else self.ffn_weights(st, l, which)
            B = dict(
                xa=fw.sbuf(st, "xa", [128, 8, TT], F32),
                hT=fw.sbuf(st, "hT", [128, 8, TT], BF16),
                msp=fw.psum(st, "msp", [128, 512], F32),
                sq=[fw.sbuf(st, f"sq{i}", [128, TT], BF16) for i in range(2)],
                tmp=[fw.sbuf(st, f"tmp{i}", [128, TT], F32) for i in range(2)],
                rstd=fw.sbuf(st, "rstd", [128, TT], F32),
                epsc=self.epsc(st),
            )
            hT = B["hT"]
            aT = fw.sbuf(st, "aT", [128, NF, TT], BF16)
            sgb = [fw.sbuf(st, f"sg{i}", [128, TT], BF16) for i in range(2)]
            xr = [fw.sbuf(st, f"xr{i}", [128, TT], F32) for i in range(3)]
            gps = [fw.psum(st, f"gps{i}", [128, 512], F32) for i in range(2)]
            ups = [fw.psum(st, f"ups{i}", [128, 512], F32) for i in range(2)]
            dps = [fw.psum(st, f"dps{i}", [128, 512], F32) for i in range(2)]

            self.prep(0, l, sub, B, "sqrt")
            nxr = 0
            for g in range(NG):
                s = self.seq_of(g * TT)
                for f in range(NF):
                    gp, up = gps[f % 2], ups[f % 2]
                    pi, pf = self.gu_piece(f)
                    for ck in range(8):
                        fw.op(fw.pe, lambda f=f, ck=ck, gp=gp: nc.tensor.matmul(
                            gp[:, :], Wg[pi][:, ck, pf * 128:(pf + 1) * 128], hT[:, ck, :], start=(ck == 0), stop=(ck == 7)),
                            reads=[Wg[pi], hT], writes=[gp])
                    for ck in range(8):
                        fw.op(fw.pe, lambda f=f, ck=ck, up=up: nc.tensor.matmul(
                            up[:, :], Wu[pi][:, ck, pf * 128:(pf + 1) * 128], hT[:, ck, :], start=(ck == 0), stop=(ck == 7)),
                            reads=[Wu[pi], hT], writes=[up])
                    sg = sgb[f % 2]
                    fw.op(fw.act, lambda gp=gp, sg=sg: nc.scalar.activation(sg[:, :], gp[:, :], AF.Silu), reads=[gp], writes=[sg])
                    fw.op(fw.dve, lambda f=f, up=up, sg=sg: nc.vector.tensor_tensor(aT[:, f, :], sg[:, :], up[:, :], ALU.mult),
                          reads=[sg, up], writes=[aT])
                if g + 1 < NG:
                    self.prep(g + 1, l, sub, B, "sqrt")
                for c in range(8):
                    dp = dps[c % 2]
                    x_ = xr[nxr % 3]
                    nxr += 1
                    xb = self.db("xT", g, c)
                    fw.dma(fw.sp, x_[:, :], self.xT[:, c, g * TT:(g + 1) * TT], reads=[xb], writes=[x_], key=x_)
                    for f in range(NF):
                        fw.op(fw.pe, lambda f=f, c=c, dp=dp: nc.tensor.matmul(
                            dp[:, :], Wd[f // 11][:, f % 11, c * 128:(c + 1) * 128], aT[:, f, :], start=(f == 0), stop=(f == NF - 1)),
                            reads=[Wd[f // 11], aT], writes=[dp])
                    fw.op(fw.dve, lambda c=c, dp=dp, x_=x_, s=s: nc.vector.scalar_tensor_tensor(
                        x_[:, :], dp[:, :], self.g5[:, l, sub, s, c:c + 1], x_[:, :], ALU.mult, ALU.add),
                        reads=[dp, x_, self.gcol], writes=[x_])
                    fw.dma(fw.pool, self.xT[:, c, g * TT:(g + 1) * TT], x_[:, :], reads=[x_], writes=[xb], key=x_)
            fw.end_phase()

    def mixer(self, l, ph="ABCD"):
        if "A" in ph:
            self.mix_A(l)
        if "B" in ph:
            self.mix_B(l)
        if "C" in ph:
            self.mix_C(l)
        if "D" in ph:
            self.mix_D(l)

    def bcast_row(self, ps, row_ap, n, dst_ap_fn, rd):
        nc, fw = self.nc, self.fw
        fw.op(fw.pe, lambda: nc.tensor.matmul(ps[:, 0:n], self.onesF[0:1, :], row_ap, start=True, stop=True),
              reads=[self.onesF] + list(rd), writes=[ps])
        dst_ap_fn(ps)

    def mix_A(self, l):
        nc, fw = self.nc, self.fw
        NG = self.T // TT
        with contextlib.ExitStack() as st:
            fw.begin_phase()
            Winq = [fw.sbuf(st, f"Win{q4}", [128, 8, 1024], BF16) for q4 in range(4)]
            for q4 in range(4):
                fw.dma(fw.pool, Winq[q4][:, :, :],
                       self.w_in[l, :, q4 * 1024:(q4 + 1) * 1024].rearrange("(c p) f -> p c f", p=128), writes=[Winq[q4]], key=Winq[q4])
            B = dict(
                xa=fw.sbuf(st, "xa", [128, 8, TT], F32),
                hT=fw.sbuf(st, "hT", [128, 8, TT], BF16),
                msp=fw.psum(st, "msp", [128, 512], F32),
                sq=[fw.sbuf(st, f"sq{i}", [128, TT], BF16) for i in range(2)],
                tmp=[fw.sbuf(st, f"tmp{i}", [128, TT], F32) for i in range(1)],
                rstd=fw.sbuf(st, "rstd", [128, TT], F32),
                epsc=self.epsc(st),
            )
            hT, msp = B["hT"], B["msp"]
            pj = [fw.psum(st, f"pj{i}", [128, 512], F32) for i in range(3)]
            bp = [fw.psum(st, f"bp{i}", [128, 512], F32) for i in range(2)]
            trp = [fw.psum(st, f"trp{i}", [128, 1024], BF16) for i in range(2)]
            c0B = fw.sbuf(st, "c0B", [128, 1024], F32)
            c1B = fw.sbuf(st, "c1B", [128, 1024], F32)
            gqB = fw.sbuf(st, "gqB", [128, 64], F32)
            gkB = fw.sbuf(st, "gkB", [128, 64], F32)
            with contextlib.ExitStack() as st2:
                lbr = fw.sbuf(st2, "lbr", [1, 1024], F32)
                fw.dma(fw.sp, lbr[:, :], self.lbD[0:1, l * 1024:(l + 1) * 1024], reads=[self.db("lbD")], writes=[lbr], key=lbr)
                for hf in range(2):
                    def wr(ps, hf=hf):
                        fw.op(fw.dve, lambda: nc.vector.tensor_scalar(c0B[:, hf * 512:(hf + 1) * 512], ps[:, :], 0.5, 0.5, ALU.mult, ALU.add),
                              reads=[ps], writes=[c0B])
                        fw.op(fw.dve, lambda: nc.vector.tensor_scalar(c1B[:, hf * 512:(hf + 1) * 512], ps[:, :], -0.5, 0.5, ALU.mult, ALU.add),
                              reads=[ps], writes=[c1B])
                    self.bcast_row(pj[hf], lbr[0:1, hf * 512:(hf + 1) * 512], 512, wr, [lbr])
                fw.barrier()
            for row, dstB, fac in ((self.gqrow, gqB, 0.125), (self.gkrow, gkB, 1.0)):
                def wr(ps, dstB=dstB, fac=fac):
                    fw.op(fw.dve, lambda: nc.vector.tensor_scalar(dstB[:, :], ps[:, 0:64], fac, None, ALU.mult),
                          reads=[ps], writes=[dstB])
                self.bcast_row(pj[2], row[0:1, l * 64:(l + 1) * 64], 64, wr, [row])
            fw.op(fw.dve, lambda: nc.vector.tensor_tensor(gkB[:, :], gkB[:, :], gqB[:, :], ALU.mult), reads=[gkB, gqB], writes=[gkB])

            def stage(name, shape, dt, n=1):
                t = [fw.sbuf(st, f"{name}{i}", shape, dt) for i in range(n)]
                return t * (2 // n)
            S_qtT = stage("S_qtT", [128, 2, 4, TT], BF16)
            S_ktT = stage("S_ktT", [128, 2, 4, TT], BF16)
            S_sgT = stage("S_sgT", [128, 4, TT], BF16)
            S_QT = stage("S_QT", [64, 8, TT], BF16)
            S_KT = stage("S_KT", [64, 8, TT], BF16)
            qs = stage("qs", [128, 512], BF16, 2)
            th = stage("th", [128, 1024], F32)
            ff = th
            lfh = stage("lfh", [128, 1024], BF16, 2)
            lfl = stage("lfl", [128, 1024], BF16, 2)
            kk = stage("kk", [128, 1024], BF16, 2)
            E = stage("E", [128, 1024], F32)
            Ei = stage("Ei", [128, 1024], F32)
            qt = stage("qt", [128, 1024], BF16)
            kt = stage("kt", [128, 1024], BF16, 2)
            vt = stage("vt", [128, 512], BF16, 2)
            sgt = stage("sgt", [128, 512], BF16, 2)
            thg = stage("thg", [128, 512], F32)
            nsq = stage("nsq", [128, 1024], F32)
            nss = stage("nss", [128, 16], F32)
            nn = nsq
            nqk = stage("nqk", [128, 1024], BF16, 2)
            nv = stage("nv", [128, 512], BF16, 2)
            dec = stage("dec", [128, 32], F32, 2)

            hTs = [B["hT"], fw.sbuf(st, "hT2", [128, 8, TT], BF16)]
            npj = [0]

            def head(g, b, nb):
                i2 = nb % 2
                hT = hTs[g % 2]
                tok = g * TT + b * 128

                def proj(cg):
                    p = pj[npj[0] % 3]
                    npj[0] += 1
                    for ck in range(8):
                        fw.op(fw.pe, lambda ck=ck: nc.tensor.matmul(
                            p[:, :], hT[:, ck, b * 128:(b + 1) * 128], Winq[cg // 2][:, ck, (cg % 2) * 512:(cg % 2 + 1) * 512],
                            start=(ck == 0), stop=(ck == 7)), reads=[hT, Winq[cg // 2]], writes=[p])
                    return p
                for dr_ in range(2):
                    p = proj(1 + dr_)
                    fw.op(fw.act, lambda p=p, dr_=dr_: nc.scalar.activation(th[0][:, dr_ * 512:(dr_ + 1) * 512], p[:, :], AF.Tanh, scale=0.5),
                          reads=[p], writes=[th[0]])
                pq = proj(0)
                fw.op(fw.act, lambda: nc.scalar.activation(thg[0][:, :], pq[:, :], AF.Tanh, scale=0.5), reads=[pq], writes=[thg[0]])
                fw.op(fw.dve, lambda: nc.vector.scalar_tensor_tensor(qs[i2][:, :], thg[0][:, :], 1.0, pq[:, :], ALU.add, ALU.mult),
                      reads=[thg[0], pq], writes=[qs[i2]])
                pg = proj(4)
                fw.op(fw.act, lambda: nc.scalar.activation(thg[0][:, :], pg[:, :], AF.Tanh, scale=0.5), reads=[pg], writes=[thg[0]])
                fw.op(fw.dve, lambda: nc.vector.scalar_tensor_tensor(sgt[i2][:, :], thg[0][:, :], 1.0, pg[:, :], ALU.add, ALU.mult),
                      reads=[thg[0], pg], writes=[sgt[i2]])
                fw.op(fw.dve, lambda: nc.vector.tensor_tensor(th[0][:, :], th[0][:, :], c1B[:, :], ALU.mult),
                      reads=[th[0], c1B], writes=[th[0]])
                fw.op(fw.dve, lambda: nc.vector.tensor_tensor(th[0][:, :], th[0][:, :], c0B[:, :], ALU.add),
                      reads=[th[0], c0B], writes=[th[0]])
                fw.op(fw.dve, lambda: nc.vector.tensor_scalar(kk[i2][:, :], th[0][:, :], -1.0, 1.0, ALU.mult, ALU.add),
                      reads=[th[0]], writes=[kk[i2]])
                fw.op(fw.act, lambda: nc.scalar.activation(th[0][:, :], th[0][:, :], AF.Ln), reads=[th[0]], writes=[th[0]])
                fw.op(fw.dve, lambda: nc.vector.tensor_copy(lfh[i2][:, :], th[0][:, :]), reads=[th[0]], writes=[lfh[i2]])
                fw.op(fw.dve, lambda: nc.vector.tensor_tensor(lfl[i2][:, :], th[0][:, :], lfh[i2][:, :], ALU.subtract),
                      reads=[th[0], lfh[i2]], writes=[lfl[i2]])
                pv = proj(3)
                fw.op(fw.act, lambda: nc.scalar.copy(vt[i2][:, :], pv[:, :]), reads=[pv], writes=[vt[i2]])
                fw.dma(fw.pool, self.vtok[tok:tok + 128, :], vt[i2][:, :], reads=[vt[i2]], writes=[self.db("vtok", g)], key=vt[i2])

            def head2(g, b, nb):
                i2 = nb % 2
                hT = hTs[g % 2]
                tok = g * TT + b * 128

                def proj(cg):
                    p = pj[npj[0] % 3]
                    npj[0] += 1
                    for ck in range(8):
                        fw.op(fw.pe, lambda ck=ck: nc.tensor.matmul(
                            p[:, :], hT[:, ck, b * 128:(b + 1) * 128], Winq[cg // 2][:, ck, (cg % 2) * 512:(cg % 2 + 1) * 512],
                            start=(ck == 0), stop=(ck == 7)), reads=[hT, Winq[cg // 2]], writes=[p])
                    return p
                pq_ = proj(5)
                fw.op(fw.act, lambda: nc.scalar.activation(nsq[0][:, 0:512], pq_[:, :], AF.Square), reads=[pq_], writes=[nsq[0]])
                pk_ = proj(6)
                fw.op(fw.act, lambda: nc.scalar.activation(nsq[0][:, 512:1024], pk_[:, :], AF.Square), reads=[pk_], writes=[nsq[0]])
                fw.op(fw.dve, lambda: nc.vector.tensor_reduce(
                    nss[0][:, 0:16], nsq[0][:, :].rearrange("p (h d) -> p h d", h=16), AX.X, ALU.add), reads=[nsq[0]], writes=[nss[0]])
                fw.op(fw.act, lambda: nc.scalar.activation(nss[0][:, 0:16], nss[0][:, 0:16], AF.Ln, bias=B["epsc"][:, 0:1], scale=1.0 / 64),
                      reads=[nss[0], B["epsc"]], writes=[nss[0]])
                fw.op(fw.act, lambda: nc.scalar.activation(nss[0][:, 0:16], nss[0][:, 0:16], AF.Exp, scale=-0.5), reads=[nss[0]], writes=[nss[0]])
                fw.op(fw.dve, lambda: nc.vector.tensor_tensor(
                    nqk[i2][:, 0:512].rearrange("p (h d) -> p h d", h=8), pq_[:, :].rearrange("p (h d) -> p h d", h=8),
                    nss[0][:, 0:8].unsqueeze(2).to_broadcast([128, 8, 64]), ALU.mult), reads=[pq_, nss[0]], writes=[nqk[i2]])
                fw.op(fw.dve, lambda: nc.vector.tensor_tensor(
                    nsq[0][:, 512:1024].rearrange("p (h d) -> p h d", h=8), pk_[:, :].rearrange("p (h d) -> p h d", h=8),
                    nss[0][:, 8:16].unsqueeze(2).to_broadcast([128, 8, 64]), ALU.mult), reads=[pk_, nss[0]], writes=[nsq[0]])
                fw.op(fw.dve, lambda: nc.vector.tensor_tensor(
                    nqk[i2][:, 512:1024].rearrange("p (h d) -> p h d", h=8),
                    nsq[0][:, 512:1024].rearrange("p (h d) -> p h d", h=8),
                    gkB[:, :].unsqueeze(1).to_broadcast([128, 8, 64]), ALU.mult),
                    reads=[nsq[0], gkB], writes=[nqk[i2]])

            def head2b(g, b, nb):
                i2 = nb % 2
                hT = hTs[g % 2]
                tok = g * TT + b * 128
                p = pj[npj[0] % 3]
                npj[0] += 1
                for ck in range(8):
                    fw.op(fw.pe, lambda ck=ck: nc.tensor.matmul(
                        p[:, :], hT[:, ck, b * 128:(b + 1) * 128], Winq[3][:, ck, 512:1024],
                        start=(ck == 0), stop=(ck == 7)), reads=[hT, Winq[3]], writes=[p])
                fw.op(fw.act, lambda: nc.scalar.copy(nv[i2][:, :], p[:, :]), reads=[p], writes=[nv[i2]])
                fw.dma(fw.pool, self.Vna[tok:tok + 128, :], nv[i2][:, :], reads=[nv[i2]], writes=[self.db("Vna", g)], key=nv[i2])

            def tail(g, b, nb):
                i2 = nb % 2
                gi = 0
                tok = g * TT + b * 128
                blk = tok // 128
                for dr_ in range(2):
                    Tm = self.TfB if dr_ == 0 else self.TbB
                    fw.op(fw.pe, lambda dr_=dr_, Tm=Tm: nc.tensor.matmul(
                        bp[dr_][:, :], Tm[:, :], lfh[i2][:, dr_ * 512:(dr_ + 1) * 512], start=True, stop=False),
                        reads=[Tm, lfh[i2]], writes=[bp[dr_]])
                    fw.op(fw.pe, lambda dr_=dr_, Tm=Tm: nc.tensor.matmul(
                        bp[dr_][:, :], Tm[:, :], lfl[i2][:, dr_ * 512:(dr_ + 1) * 512], start=False, stop=True),
                        reads=[Tm, lfl[i2]], writes=[bp[dr_]])
                    fw.op(fw.act, lambda dr_=dr_: nc.scalar.activation(E[0][:, dr_ * 512:(dr_ + 1) * 512], bp[dr_][:, :], AF.Exp),
                          reads=[bp[dr_]], writes=[E[0]])
                    fw.op(fw.act, lambda dr_=dr_: nc.scalar.activation(Ei[0][:, dr_ * 512:(dr_ + 1) * 512], bp[dr_][:, :], AF.Exp, scale=-1.0),
                          reads=[bp[dr_]], writes=[Ei[0]])
                    fw.op(fw.dve, lambda dr_=dr_: nc.vector.scalar_tensor_tensor(
                        qt[0][:, dr_ * 512:(dr_ + 1) * 512], qs[i2][:, :], 0.5, E[0][:, dr_ * 512:(dr_ + 1) * 512], ALU.mult, ALU.mult),
                        reads=[qs[i2], E[0]], writes=[qt[0]])
                fw.op(fw.dve, lambda: nc.vector.tensor_tensor(kt[i2][:, :], kk[i2][:, :], Ei[0][:, :], ALU.mult),
                      reads=[kk[i2], Ei[0]], writes=[kt[i2]])
                for hd in range(8):
                    fw.op(fw.pe, lambda hd=hd: nc.tensor.matmul(
                        msp[:, hd * 4:(hd + 1) * 4], lfh[i2][:, hd * 128:(hd + 1) * 128], self.IndB[:, :], start=True, stop=False),
                        reads=[lfh[i2], self.IndB], writes=[msp])
                    fw.op(fw.pe, lambda hd=hd: nc.tensor.matmul(
                        msp[:, hd * 4:(hd + 1) * 4], lfl[i2][:, hd * 128:(hd + 1) * 128], self.IndB[:, :], start=False, stop=True),
                        reads=[lfl[i2], self.IndB], writes=[msp])
                fw.op(fw.act, lambda: nc.scalar.activation(dec[i2][:, :], msp[:, 0:32], AF.Exp), reads=[msp], writes=[dec[i2]])
                fw.dma(fw.pool, self.decT[blk, :, :], dec[i2][:, :], reads=[dec[i2]], writes=[self.db("decT", g)], key=dec[i2])
                for dr_ in range(2):
                    fw.dma(fw.pool, self.ktok[dr_, tok:tok + 128, :], kt[i2][:, dr_ * 512:(dr_ + 1) * 512], reads=[kt[i2]],
                           writes=[self.db("ktok", g)], key=kt[i2])

            def tail2(g, b, nb):
                i2 = nb % 2
                gi = 0
                tok = g * TT + b * 128
                for src, dstS, tp in ((qt[0], S_qtT, trp[0]), (kt[i2], S_ktT, trp[1])):
                    for hd in range(8):
                        fw.op(fw.pe, lambda hd=hd, src=src, tp=tp: nc.tensor.transpose(
                            tp[:, hd * 128:(hd + 1) * 128], src[:, hd * 128:(hd + 1) * 128], self.identB[:, :]),
                            reads=[src, self.identB], writes=[tp])
                    dsta = dstS[gi][:, :, :, b * 128:(b + 1) * 128]
                    srca = tp[:, :].rearrange("p (d h t) -> p d h t", d=2, h=4)
                    if src is qt[0]:
                        fw.op(fw.dve, lambda dsta=dsta, srca=srca: nc.vector.tensor_copy(dsta, srca), reads=[tp], writes=[dstS[gi]])
                    else:
                        fw.op(fw.act, lambda dsta=dsta, srca=srca: nc.scalar.copy(dsta, srca), reads=[tp], writes=[dstS[gi]])

            def tail2b(g, b, nb):
                i2 = nb % 2
                gi = 0
                bpb = bp[0][:, :].bitcast(BF16)
                for j in range(4):
                    fw.op(fw.pe, lambda j=j: nc.tensor.transpose(bpb[:, j * 128:(j + 1) * 128], sgt[i2][:, j * 128:(j + 1) * 128], self.identB[:, :]),
                          reads=[sgt[i2], self.identB], writes=[bp[0]])
                fw.op(fw.dve, lambda: nc.vector.tensor_copy(S_sgT[gi][:, :, b * 128:(b + 1) * 128],
                                                            bpb[:, 0:512].rearrange("p (h t) -> p h t", h=4)),
                      reads=[bp[0]], writes=[S_sgT[gi]])
                for w_, tp, dstS in ((0, trp[1], S_QT), (1, trp[0], S_KT)):
                    for h in range(8):
                        fw.op(fw.pe, lambda h=h, w_=w_, tp=tp: nc.tensor.transpose(
                            tp[0:64, h * 128:(h + 1) * 128], nqk[i2][:, w_ * 512 + h * 64:w_ * 512 + (h + 1) * 64], self.identB[:, :]),
                            reads=[nqk[i2], self.identB], writes=[tp])
                    if w_ == 0:
                        fw.op(fw.act, lambda tp=tp, dstS=dstS: nc.scalar.copy(
                            dstS[gi][:, :, b * 128:(b + 1) * 128], tp[0:64, :].rearrange("p (h t) -> p h t", h=8)),
                            reads=[tp], writes=[dstS[gi]])
                    else:
                        fw.op(fw.dve, lambda tp=tp, dstS=dstS: nc.vector.tensor_copy(
                            dstS[gi][:, :, b * 128:(b + 1) * 128], tp[0:64, :].rearrange("p (h t) -> p h t", h=8)),
                            reads=[tp], writes=[dstS[gi]])
                if b == 3:
                    tsl = slice(g * TT, (g + 1) * TT)
                    for dr_ in range(2):
                        fw.dma(fw.pool, self.qtT[dr_, :, :, tsl], S_qtT[gi][:, dr_, :, :], reads=[S_qtT[gi]], writes=[self.db("qtT", g)], key=S_qtT[gi])
                        fw.dma(fw.pool, self.ktT[dr_, :, :, tsl], S_ktT[gi][:, dr_, :, :], reads=[S_ktT[gi]], writes=[self.db("ktT", g)], key=S_ktT[gi])
                    fw.dma(fw.pool, self.sgT[:, :, tsl], S_sgT[gi][:, :, :], reads=[S_sgT[gi]], writes=[self.db("sgT", g)], key=S_sgT[gi])
                    fw.dma(fw.pool, self.QT[:, :, tsl], S_QT[gi][:, :, :], reads=[S_QT[gi]], writes=[self.db("QT", g)], key=S_QT[gi])
                    fw.dma(fw.pool, self.KT[:, :, tsl], S_KT[gi][:, :, :], reads=[S_KT[gi]], writes=[self.db("KT", g)], key=S_KT[gi])

            seqb = [(g, b) for g in range(NG) for b in range(4)]
            B["hT"] = hTs[0]
            self.prep(0, l, 1, B, "lnexp")
            def prep_piece(g2, b2):
                if g2 + 1 >= NG:
                    return
                if b2 == 0:
                    B["hT"] = hTs[(g2 + 1) % 2]
                    self.prep(g2 + 1, l, 1, B, "lnexp", stats_only=True)
                self.prep_mod(g2 + 1, l, 1, B, "lnexp", [2 * b2, 2 * b2 + 1])

            head(seqb[0][0], seqb[0][1], 0)
            head2(seqb[0][0], seqb[0][1], 0)
            head2b(seqb[0][0], seqb[0][1], 0)
            prep_piece(0, 0)
            for i, (g, b) in enumerate(seqb):
                tail(g, b, i)
                if i + 1 < len(seqb):
                    g2, b2 = seqb[i + 1]
                    head(g2, b2, i + 1)
                tail2(g, b, i)
                if i + 1 < len(seqb):
                    head2(g2, b2, i + 1)
                tail2b(g, b, i)
                if i + 1 < len(seqb):
                    head2b(g2, b2, i + 1)
                    prep_piece(g2, b2)
            fw.end_phase()

    def mix_B(self, l):
        nc, fw = self.nc, self.fw
        NS = self.NS
        groups, cur, tot = [], [], 0
        for s in range(NS):
            if cur and tot + self.seqs[s] > 4096:
                groups.append(cur)
                cur, tot = [], 0
            cur.append(s)
            tot += self.seqs[s]
        groups.append(cur)
        for grp in groups:
            self.mix_B_group(l, grp)

    def mix_B_group(self, l, grp):
        nc, fw = self.nc, self.fw
        NS = self.NS
        with contextlib.ExitStack() as st:
            fw.begin_phase()
            epsc = self.epsc(st)
            chains = []
            for s in grp:
                for dr_ in range(2):
                    ch = dict(
                        s=s, dr=dr_,
                        S=fw.sbuf(st, f"S{s}{dr_}", [128, 512], F32),
                        Sb=fw.sbuf(st, f"Sb{s}{dr_}", [128, 512], BF16),
                        qT=[fw.sbuf(st, f"bq{s}{dr_}{i}", [128, 4, 128], BF16) for i in range(2)],
                        kT=[fw.sbuf(st, f"bk{s}{dr_}{i}", [128, 4, 128], BF16) for i in range(2)],
                        kt=[fw.sbuf(st, f"bkt{s}{dr_}{i}", [32, 4, 512], BF16) for i in range(2)],
                        v=[fw.sbuf(st, f"bv{s}{dr_}{i}", [32, 4, 512], BF16) for i in range(2)],
                        dc=[fw.sbuf(st, f"bd{s}{dr_}{i}", [128, 32], F32) for i in range(2)],
                        A=[fw.sbuf(st, f"bA{s}{dr_}{i}", [32, 128], BF16) for i in range(2)],
                        tmp=fw.sbuf(st, f"bt{s}{dr_}", [128, 512], F32),
                    )
                    fw.op(fw.dve, lambda ch=ch: nc.vector.memset(ch["S"][:, :], 0.0), writes=[ch["S"]])
                    fw.op(fw.pool, lambda ch=ch: nc.gpsimd.memset(ch["Sb"][:, :], 0.0), writes=[ch["Sb"]])
                    chains.append(ch)
            ops = [fw.psum(st, f"bo{i}", [128, 512], F32) for i in range(4)]
            aps = [fw.psum(st, f"ba{i}", [128, 512], F32) for i in range(2)]
            sps = [fw.psum(st, f"bs{i}", [128, 512], F32) for i in range(2)]
            oacc = {s: fw.sbuf(st, f"oacc{s}", [128, 4, self.seqs[s]], F32) for s in grp}
            maxnb = max(self.seqs[s] for s in grp) // 128
            cnt = dict(a=0, s=0)
            nch = len(chains)
            for step in range(maxnb):
                act = []
                for j, ch in enumerate(chains):
                    s, dr_ = ch["s"], ch["dr"]
                    nbk = self.seqs[s] // 128
                    if step >= nbk:
                        continue
                    bi = step if dr_ == 0 else nbk - 1 - step
                    tok = self.seq_off[s] + bi * 128
                    g = tok // TT
                    i2 = step % 2
                    ch["cur"] = dict(qT=ch["qT"][i2], kT=ch["kT"][i2], kt=ch["kt"][i2], v=ch["v"][i2], dc=ch["dc"][i2],
                                     bi=bi, nbk=nbk, op=ops[(j + nch * (step % 2)) % 4 if 2 * nch <= 4 else j])
                    c_ = ch["cur"]
                    fw.dma(fw.sp, c_["qT"][:, :, :], self.qtT[dr_, :, :, tok:tok + 128], reads=[self.db("qtT", g)], writes=[c_["qT"]], key=c_["qT"])
                    fw.dma(fw.sp, c_["kT"][:, :, :], self.ktT[dr_, :, :, tok:tok + 128], reads=[self.db("ktT", g)], writes=[c_["kT"]], key=c_["kT"])
                    fw.dma(fw.sp, c_["kt"][:, :, :], self.ktok[dr_, tok:tok + 128, :].rearrange("(c j) f -> j c f", j=32),
                           reads=[self.db("ktok", g)], writes=[c_["kt"]], key=c_["kt"])
                    fw.dma(fw.sp, c_["v"][:, :, :], self.vtok[tok:tok + 128, :].rearrange("(c j) f -> j c f", j=32),
                           reads=[self.db("vtok", g)], writes=[c_["v"]], key=c_["v"])
                    fw.dma(fw.sp, c_["dc"][:, :], self.decT[tok // 128, :, :], reads=[self.db("decT", g)], writes=[c_["dc"]], key=c_["dc"])
                    act.append(ch)
                for ci in range(4):
                    for ch in act:
                        s, dr_ = ch["s"], ch["dr"]
                        c_ = ch["cur"]
                        qT, kT, kt, v, dc, op_ = c_["qT"], c_["kT"], c_["kt"], c_["v"], c_["dc"], c_["op"]
                        c = ci if dr_ == 0 else 3 - ci
                        t0 = c * 32
                        ap_ = aps[cnt["a"] % 2]
                        cnt["a"] += 1
                        sp_ = sps[cnt["s"] % 2]
                        cnt["s"] += 1
                        A = ch["A"][ci % 2]
                        for h in range(4):
                            fw.op(fw.pe, lambda h=h: nc.tensor.matmul(
                                ap_[0:32, h * 32:(h + 1) * 32], kT[:, h, t0:t0 + 32], qT[:, h, t0:t0 + 32], start=True, stop=True),
                                reads=[kT, qT], writes=[ap_])
                        fw.op(fw.dve, lambda: nc.vector.tensor_tensor(
                            A[:, :], ap_[0:32, 0:128], self.maskA[:, dr_ * 128:(dr_ + 1) * 128], ALU.mult),
                            reads=[ap_, self.maskA], writes=[A])
                        for h in range(4):
                            fw.op(fw.pe, lambda h=h: nc.tensor.matmul(
                                sp_[:, h * 128:(h + 1) * 128], kt[:, c, h * 128:(h + 1) * 128], v[:, c, h * 128:(h + 1) * 128],
                                start=True, stop=True), reads=[kt, v], writes=[sp_])
                        for h in range(4):
                            osl = op_[:, h * 128 + t0:h * 128 + t0 + 32]
                            fw.op(fw.pe, lambda h=h, osl=osl: nc.tensor.matmul(
                                osl, ch["Sb"][:, h * 128:(h + 1) * 128], qT[:, h, t0:t0 + 32], start=True, stop=False),
                                reads=[ch["Sb"], qT], writes=[op_])
                            fw.op(fw.pe, lambda h=h, osl=osl: nc.tensor.matmul(
                                osl, v[:, c, h * 128:(h + 1) * 128], A[:, h * 32:(h + 1) * 32], start=False, stop=True),
                                reads=[v, A], writes=[op_])
                        fw.op(fw.dve, lambda: nc.vector.tensor_tensor(ch["tmp"][:, :], ch["S"][:, :], sp_[:, :], ALU.add),
                              reads=[ch["S"], sp_], writes=[ch["tmp"]])
                        dsl = dc[:, dr_ * 16:(dr_ + 1) * 16].rearrange("p (h c) -> p h c", h=4)[:, :, c:c + 1].to_broadcast([128, 4, 128])
                        fw.op(fw.pool, lambda: nc.gpsimd.tensor_tensor(
                            ch["S"][:, :].rearrange("p (h v) -> p h v", h=4), ch["tmp"][:, :].rearrange("p (h v) -> p h v", h=4), dsl, ALU.mult),
                            reads=[ch["tmp"], dc], writes=[ch["S"]])
                        fw.op(fw.act, lambda: nc.scalar.copy(ch["Sb"][:, :], ch["S"][:, :]), reads=[ch["S"]], writes=[ch["Sb"]])
                for ch in act:
                    s, dr_ = ch["s"], ch["dr"]
                    c_ = ch["cur"]
                    bi, nbk, op_ = c_["bi"], c_["nbk"], c_["op"]
                    osb = oacc[s][:, :, bi * 128:(bi + 1) * 128]
                    o3 = op_[:, :].rearrange("p (h t) -> p h t", h=4)
                    fwd_first = bi <= (nbk - 1 - bi)
                    is_first = ((dr_ == 0) == fwd_first) if bi != nbk - 1 - bi else (dr_ == 0)
                    if is_first:
                        fw.op(fw.act, lambda: nc.scalar.copy(osb, o3), reads=[op_], writes=[oacc[s]])
                    else:
                        fw.op(fw.dve, lambda: nc.vector.tensor_tensor(osb, osb, o3, ALU.add),
                              reads=[op_, oacc[s]], writes=[oacc[s]])
            sqb = [fw.sbuf(st, f"fsq{i}", [128, TT], BF16) for i in range(2)]
            rs = [fw.sbuf(st, f"frs{i}", [128, TT], F32) for i in range(2)]
            sgl = [fw.sbuf(st, f"fsg{i}", [128, 4, TT], BF16) for i in range(2)]
            oo = [fw.sbuf(st, f"foo{i}", [128, 4, TT], BF16) for i in range(2)]
            on = [fw.sbuf(st, f"fon{i}", [128, TT], F32) for i in range(2)]
            k = 0
            for g in range(self.T // TT):
                s = self.seq_of(g * TT)
                if s not in grp:
                    continue
                lt = g * TT - self.seq_off[s]
                gi = g % 2
                fw.dma(fw.sp, sgl[gi][:, :, :], self.sgT[:, :, g * TT:(g + 1) * TT], reads=[self.db("sgT", g)], writes=[sgl[gi]], key=sgl[gi])
                for h in range(4):
                    k += 1
                    o_ = oacc[s][:, h, lt:lt + TT]
                    sq, r_, n_ = sqb[k % 2], rs[k % 2], on[k % 2]
                    mp = ops[k % 3]
                    fw.op(fw.act, lambda o_=o_, sq=sq: nc.scalar.activation(sq[:, :], o_, AF.Square), reads=[oacc[s]], writes=[sq])
                    fw.op(fw.pe, lambda sq=sq, mp=mp: nc.tensor.matmul(mp[:, :], self.onesB[:, :], sq[:, :], start=True, stop=True),
                          reads=[sq, self.onesB], writes=[mp])
                    fw.op(fw.act, lambda mp=mp, r_=r_: nc.scalar.activation(r_[:, :], mp[:, :], AF.Ln, bias=epsc[:, 0:1], scale=1.0 / 128),
                          reads=[mp, epsc], writes=[r_])
                    fw.op(fw.act, lambda r_=r_: nc.scalar.activation(r_[:, :], r_[:, :], AF.Exp, scale=-0.5), reads=[r_], writes=[r_])
                    fw.op(fw.dve, lambda o_=o_, r_=r_, n_=n_: nc.vector.tensor_tensor(n_[:, :], o_, r_[:, :], ALU.mult),
                          reads=[oacc[s], r_], writes=[n_])
                    fw.op(fw.dve, lambda n_=n_, h=h, gi=gi: nc.vector.scalar_tensor_tensor(
                        oo[gi][:, h, :], n_[:, :], self.ghgT[:, l:l + 1], sgl[gi][:, h, :], ALU.mult, ALU.mult),
                        reads=[n_, self.ghgT, sgl[gi]], writes=[oo[gi]])
                fw.dma(fw.pool, self.ohgT[:, :, g * TT:(g + 1) * TT], oo[gi][:, :, :], reads=[oo[gi]], writes=[self.db("ohgT", g)], key=oo[gi])
            fw.end_phase()

    def mix_C(self, l):
        nc, fw = self.nc, self.fw
        import os
        LVL = int(os.environ.get("NA_LVL", "9"))
        with contextlib.ExitStack() as st:
            fw.begin_phase()
            G2 = fw.sbuf(st, "G2", [128, 8 * 14 * 64], F32)
            fw.dma(fw.sp, G2[:, :], self.G[l, :, :], writes=[G2], key=G2)
            G4 = G2[:, :].rearrange("p (h d q) -> p h d q", h=8, d=14)
            if LVL >= 0:
              fw.op(fw.dve, lambda: nc.vector.tensor_tensor(
                G2[:, :].rearrange("p (a q) -> p a q", q=64), G2[:, :].rearrange("p (a q) -> p a q", q=64),
                self.NEG[:, :].unsqueeze(1).to_broadcast([128, 8 * 14, 64]), ALU.add), reads=[G2, self.NEG], writes=[G2])
            Lmax = max(self.seqs)
            QTs = fw.sbuf(st, "QTs", [64, 8, Lmax], BF16)
            KTs = fw.sbuf(st, "KTs", [64, 8, Lmax], BF16)
            Vw = [fw.sbuf(st, f"Vw{i}", [128, 4, 512], BF16) for i in range(3)]
            sb = [fw.sbuf(st, f"nsb{i}", [128, 512], F32) for i in range(2)]
            pT = [fw.sbuf(st, f"npT{i}", [128, 512], BF16) for i in range(2)]
            rec = fw.sbuf(st, "nrec", [64, 512], F32)
            ost = [fw.sbuf(st, f"nost{i}", [64, 8, TT], BF16) for i in range(2)]
            stp = [fw.psum(st, f"nst{i}", [128, 512], F32) for i in range(4)]
            nump = [fw.psum(st, f"nnum{i}", [128, 512], F32) for i in range(2)]
            denp = [fw.psum(st, f"nden{i}", [128, 512], F32) for i in range(2)]
            items = []
            nrow = 0
            for s in range(self.NS):
                L = self.seqs[s]
                off = self.seq_off[s]
                rows = L // 64
                for r in range(rows):
                    r0 = min(max(r - 4, 0), rows - 8)
                    for hp in range(4):
                        items.append(dict(s=s, L=L, off=off, rows=rows, r=r, r0=r0, dl=r - r0, hp=hp, nrow=nrow,
                                          k=len(items) + 1, gi=(off + r * 64) // TT))
                    nrow += 1

            def stage1(it):
                s, L, off, r, r0, hp = it["s"], it["L"], it["off"], it["r"], it["r0"], it["hp"]
                if hp == 0:
                    if r == 0:
                        gs = [g for g in range(off // TT, (off + L) // TT)]
                        fw.dma(fw.sp, QTs[:, :, 0:L], self.QT[:, :, off:off + L], reads=[self.db("QT", g) for g in gs], writes=[QTs], key=QTs)
                        fw.dma(fw.sp, KTs[:, :, 0:L], self.KT[:, :, off:off + L], reads=[self.db("KT", g) for g in gs], writes=[KTs], key=KTs)
                    vw = Vw[it["nrow"] % 3]
                    t0 = off + r0 * 64
                    fw.dma(fw.sp, vw[:, :, :], self.Vna[t0:t0 + 512, :].rearrange("(m p) f -> p m f", p=128),
                           reads=[self.db("Vna", g) for g in range(t0 // TT, (t0 + 511) // TT + 1)], writes=[vw], key=vw)
                stq = stp[it["k"] % 4]
                for h2 in range(2):
                    for m in range(4):
                        fw.op(fw.pe, lambda h2=h2, m=m: nc.tensor.matmul(
                            stq[:, (h2 * 4 + m) * 64:(h2 * 4 + m + 1) * 64],
                            KTs[:, 2 * hp + h2, r0 * 64 + m * 128:r0 * 64 + (m + 1) * 128],
                            QTs[:, 2 * hp + h2, r * 64:(r + 1) * 64], start=True, stop=True),
                            reads=[KTs, QTs], writes=[stq])

            def stage2(it):
                off, r, hp, dl, k = it["off"], it["r"], it["hp"], it["dl"], it["k"]
                vw = Vw[it["nrow"] % 3]
                np_, dp_ = nump[it["nrow"] % 2], denp[it["nrow"] % 2]
                gi = it["gi"]
                og = ost[gi % 2]
                stq = stp[k % 4]
                sb_, pT_ = sb[k % 2], pT[k % 2]
                fw.op(fw.dve, lambda: nc.vector.tensor_tensor(
                    sb_[:, :].rearrange("p (h m q) -> p h m q", h=2, m=4),
                    stq[:, :].rearrange("p (h m q) -> p h m q", h=2, m=4),
                    G4[:, 2 * hp:2 * hp + 2, 7 - dl:14 - dl:2, :], ALU.add), reads=[stq, G2], writes=[sb_])
                fw.op(fw.act, lambda: nc.scalar.activation(pT_[:, :], sb_[:, :], AF.Exp), reads=[sb_], writes=[pT_])
                for h2 in range(2):
                    h = 2 * hp + h2
                    for m in range(4):
                        fw.op(fw.pe, lambda h=h, h2=h2, m=m: nc.tensor.matmul(
                            np_[0:64, h * 64:(h + 1) * 64], vw[:, m, h * 64:(h + 1) * 64],
                            pT_[:, (h2 * 4 + m) * 64:(h2 * 4 + m + 1) * 64], start=(m == 0), stop=(m == 3)),
                            reads=[vw, pT_], writes=[np_])
                for m in range(4):
                    fw.op(fw.pe, lambda m=m: nc.tensor.matmul(
                        dp_[0:64, hp * 128:(hp + 1) * 128].rearrange("p (h q) -> p h q", h=2),
                        self.onesB[:, 0:64],
                        pT_[:, :].rearrange("p (h m q) -> p h m q", h=2, m=4)[:, :, m, :], start=(m == 0), stop=(m == 3)),
                        reads=[self.onesB, pT_], writes=[dp_])

            def row_end(it):
                off, r = it["off"], it["r"]
                np_, dp_ = nump[it["nrow"] % 2], denp[it["nrow"] % 2]
                gi = it["gi"]
                og = ost[gi % 2]
                fw.op(fw.act, lambda: nc.scalar.activation(rec[:, :], dp_[0:64, :], AF.Ln), reads=[dp_], writes=[rec])
                fw.op(fw.act, lambda: nc.scalar.activation(rec[:, :], rec[:, :], AF.Exp, scale=-1.0), reads=[rec], writes=[rec])
                tl = (off + r * 64) - gi * TT
                fw.op(fw.dve, lambda: nc.vector.tensor_tensor(
                    og[:, :, tl:tl + 64], np_[0:64, :].rearrange("p (h q) -> p h q", h=8),
                    rec[:, :].rearrange("p (h q) -> p h q", h=8), ALU.mult), reads=[np_, rec], writes=[og])
                if tl + 64 == TT:
                    for par in range(2):
                        fw.dma(fw.pool, self.onaT[par * 64:(par + 1) * 64, :, gi * TT:(gi + 1) * TT], og[:, par::2, :],
                               reads=[og], writes=[self.db("onaT", gi)], key=og)

            stage1(items[0])
            if len(items) > 1:
                stage1(items[1])
            LAG = 2
            for i, it in enumerate(items):
                if i + 2 < len(items):
                    stage1(items[i + 2])
                stage2(it)
                if i - LAG >= 0 and items[i - LAG]["hp"] == 3:
                    row_end(items[i - LAG])
            for j in range(max(0, len(items) - LAG), len(items)):
                if items[j]["hp"] == 3:
                    row_end(items[j])
            fw.end_phase()

    def mix_D(self, l):
        nc, fw = self.nc, self.fw
        with contextlib.ExitStack() as st:
            fw.begin_phase()
            Wo = fw.sbuf(st, "Wo", [128, 8, D], BF16)
            fw.dma(fw.pool, Wo[:, :, :], self.w_out[l, :, :].rearrange("(c p) d -> p c d", p=128), writes=[Wo], key=Wo)
            oh = [fw.sbuf(st, f"doh{i}", [128, 4, TT], BF16) for i in range(2)]
            on_ = [fw.sbuf(st, f"don{i}", [128, 4, TT], BF16) for i in range(2)]
            xr = [fw.sbuf(st, f"dxr{i}", [128, TT], F32) for i in range(4)]
            dps = [fw.psum(st, f"dd{i}", [128, 512], F32) for i in range(4)]
            nxr = 0
            for g in range(self.T // TT):
                s = self.seq_of(g * TT)
                gi = g % 2
                tsl = slice(g * TT, (g + 1) * TT)
                fw.dma(fw.sp, oh[gi][:, :, :], self.ohgT[:, :, tsl], reads=[self.db("ohgT", g)], writes=[oh[gi]], key=oh[gi])
                fw.dma(fw.sp, on_[gi][:, :, :], self.onaT[:, :, tsl], reads=[self.db("onaT", g)], writes=[on_[gi]], key=on_[gi])
                for c in range(8):
                    dp = dps[nxr % 4]
                    x_ = xr[nxr % 4]
                    nxr += 1
                    xb = self.db("xT", g, c)
                    fw.dma(fw.sp, x_[:, :], self.xT[:, c, tsl], reads=[xb], writes=[x_], key=x_)
                    for kc in range(8):
                        src = oh[gi] if kc < 4 else on_[gi]
                        fw.op(fw.pe, lambda kc=kc, src=src: nc.tensor.matmul(
                            dp[:, :], Wo[:, kc, c * 128:(c + 1) * 128], src[:, kc % 4, :], start=(kc == 0), stop=(kc == 7)),
                            reads=[Wo, src], writes=[dp])
                    fw.op(fw.dve, lambda: nc.vector.scalar_tensor_tensor(
                        x_[:, :], dp[:, :], self.g5[:, l, 1, s, c:c + 1], x_[:, :], ALU.mult, ALU.add),
                        reads=[dp, x_, self.gcol], writes=[x_])
                    fw.dma(fw.act, self.xT[:, c, tsl], x_[:, :], reads=[x_], writes=[xb], key=x_)
            fw.end_phase()


def _core_inputs(inp, xcore, ccore, depth, consts, G):
    d = {
        "x": np.ascontiguousarray(xcore, dtype=np.float32),
        "c": np.ascontiguousarray(ccore, dtype=np.float32).reshape(-1, 128),
        "w_mod": inp["w_mod"][:depth],
        "b_mod": np.ascontiguousarray(inp["b_mod"][:depth]).reshape(-1, 128),
        "norm_g": np.ascontiguousarray(inp["norm_g"][:depth]).reshape(-1, 128),
        "ffn_w_gate": inp["ffn_w_gate"][:depth],
        "ffn_w_up": inp["ffn_w_up"][:depth],
        "ffn_w_down": inp["ffn_w_down"][:depth],
        "w_in": inp["w_in"][:depth],
        "w_out": inp["w_out"][:depth],
        "hg_lb_fwd": np.ascontiguousarray(inp["hg_lb_fwd"][:depth]).reshape(1, -1),
        "hg_lb_bwd": np.ascontiguousarray(inp["hg_lb_bwd"][:depth]).reshape(1, -1),
        "hg_norm_g": inp["hg_norm_g"][:depth],
        "na_q_norm_g": np.ascontiguousarray(inp["na_q_norm_g"][:depth]).reshape(1, -1),
        "na_k_norm_g": np.ascontiguousarray(inp["na_k_norm_g"][:depth]).reshape(1, -1),
        "G": G[:depth].reshape(depth, 128, -1),
    }
    for k, v in consts.items():
        d["c_" + k] = v
    return {k: np.ascontiguousarray(np.asarray(v, dtype=np.float32)) for k, v in d.items()}


def kernel(**inputs):
    inp = {k: np.asarray(v) for k, v in inputs.items()}
    xp, xs = inp["x_prompt"], inp["x_sample"]
    cp, cs_ = inp["c_prompt"], inp["c_sample"]
    seqs = [xp.shape[1], xp.shape[1], xs.shape[1]]
    prog = Prog(seqs, DEPTH)
    nc = prog.build()
    consts = _consts()
    G = _expand_rpb(inp["na_rpb"].astype(np.float32))
    in_maps = []
    for i in range(N_CORES):
        xcore = np.concatenate([xp[2 * i], xp[2 * i + 1], xs[i]], axis=0)
        ccore = np.stack([cp[2 * i], cp[2 * i + 1], cs_[i]], axis=0)
        in_maps.append(_core_inputs(inp, xcore, ccore, DEPTH, consts, G))
    res = run_bass_kernel_spmd(nc, in_maps, core_ids=list(range(N_CORES)))
    Lp, Ls = xp.shape[1], xs.shape[1]
    yp = np.empty(xp.shape, np.float32)
    ys = np.empty(xs.shape, np.float32)
    for i in range(N_CORES):
        y = np.asarray(res.results[i]["y"], dtype=np.float32)
        yp[2 * i] = y[0:Lp]
        yp[2 * i + 1] = y[Lp:2 * Lp]
        ys[i] = y[2 * Lp:2 * Lp + Ls]
    return (yp, ys)
```
